# Optimizing a Trainium2 kernel written in Bass

```python
import jax, jax.numpy as jnp
from jax import lax
import numpy as np

D_MODEL = 1024
BATCH = 8
SEQ = 4096
DEPTH = 4
DEC_BATCH = 32
DEC_SEQ = 64
PAST_LEN = 2048

CHUNK = 64
N_MIXERS = 3
N_A = (DEPTH + 2) // 3
N_B = (DEPTH + 1) // 3
N_C = DEPTH // 3
NORM_EPS = 1e-6
NEG_INF = -1e30

A_HEADS = 8
A_HEAD_DIM = D_MODEL // A_HEADS
A_WIDTH = A_HEADS * A_HEAD_DIM
A_BAND_CHUNKS = 9
A_REACH = (A_BAND_CHUNKS - 1) * CHUNK
A_REL_CLIP = 128

B_HEADS = 16
B_KV_HEADS = 4
B_HEAD_DIM = D_MODEL // B_HEADS
B_GROUP = B_HEADS // B_KV_HEADS
B_WIDTH = B_HEADS * B_HEAD_DIM
B_KV_WIDTH = B_KV_HEADS * B_HEAD_DIM
B_WINDOW = 128
B_BAND_CHUNKS = B_WINDOW // CHUNK + 1
B_REACH = B_WINDOW

C_HEADS = 16
C_NOPE = 64
C_ROPE = 32
C_QK = C_NOPE + C_ROPE
C_V = 64
C_Q_LORA = 512
C_KV_LORA = 256
C_WIDTH = C_HEADS * C_V
C_QBLOCK = 128
ROPE_THETA = 10000.0

kernel_name = 'hybrid_stream_encoder_step'


def rmsnorm(x, g):
    xf = x.astype(jnp.float32)
    y = xf * lax.rsqrt(jnp.mean(xf * xf, axis=-1, keepdims=True) + NORM_EPS)
    return (y * g.astype(jnp.float32)).astype(x.dtype)


def rope(x, pos):
    half = x.shape[-1] // 2
    inv = ROPE_THETA ** (-jnp.arange(half, dtype=jnp.float32) / half)
    ang = pos.astype(jnp.float32)[:, None] * inv[None, :]
    shape = (1, x.shape[1]) + (1,) * (x.ndim - 3) + (half,)
    cos = jnp.cos(ang).reshape(shape).astype(x.dtype)
    sin = jnp.sin(ang).reshape(shape).astype(x.dtype)
    x1, x2 = x[..., :half], x[..., half:]
    return jnp.concatenate([x1 * cos - x2 * sin, x1 * sin + x2 * cos], axis=-1)


def alibi_slopes(n):
    return 2.0 ** (-8.0 * jnp.arange(1, n + 1, dtype=jnp.float32) / n)


def attend(q, k, v, scale, bias=None, mask=None, sinks=None):
    s = jnp.einsum('bqhgd,bkhd->bhgqk', q, k).astype(jnp.float32) * scale
    if bias is not None:
        s = s + bias
    if mask is not None:
        s = jnp.where(mask, s, NEG_INF)
    if sinks is None:
        p = jax.nn.softmax(s, axis=-1)
    else:
        sk = sinks.astype(jnp.float32)[None, :, :, None, None]
        m = jnp.maximum(jnp.max(s, axis=-1, keepdims=True), sk)
        e = jnp.exp(s - m)
        p = e / (jnp.sum(e, axis=-1, keepdims=True) + jnp.exp(sk - m))
    return jnp.einsum('bhgqk,bkhe->bqhge', p.astype(v.dtype), v)


def band_attend_prompt(q, k, v, n_band, bias_fn, scale, sinks=None):
    n, s = q.shape[0], q.shape[1]
    pad = (n_band - 1) * CHUNK
    span = n_band * CHUNK
    kp = jnp.pad(k, ((0, 0), (pad, 0), (0, 0), (0, 0)))
    vp = jnp.pad(v, ((0, 0), (pad, 0), (0, 0), (0, 0)))
    kpos = jnp.arange(span)
    bias = bias_fn((pad + jnp.arange(CHUNK))[:, None] - kpos[None, :])

    def one_chunk(c):
        start = c * CHUNK
        qb = lax.dynamic_slice_in_dim(q, start, CHUNK, axis=1)
        kb = lax.dynamic_slice_in_dim(kp, start, span, axis=1)
        vb = lax.dynamic_slice_in_dim(vp, start, span, axis=1)
        valid = (start - pad + kpos) >= 0
        return attend(qb, kb, vb, scale, bias, valid[None, :], sinks)

    out = lax.map(one_chunk, jnp.arange(s // CHUNK))
    return jnp.moveaxis(out, 0, 1).reshape((n, s) + out.shape[3:])


def band_attend_sample(q, k_cache, v_cache, k_new, v_new, bias_fn, scale, sinks=None):
    k = jnp.concatenate([k_cache, k_new], axis=1)
    v = jnp.concatenate([v_cache, v_new], axis=1)
    l, t = k_cache.shape[1], q.shape[1]
    d = (l + jnp.arange(t))[:, None] - jnp.arange(l + t)[None, :]
    return attend(q, k, v, scale, bias_fn(d), None, sinks), k, v


def chunk_causal_attend_prompt(q, k, v, scale):
    n, s = q.shape[0], q.shape[1]
    kchunk = jnp.arange(s) // CHUNK

    def one_block(i):
        start = i * C_QBLOCK
        qb = lax.dynamic_slice_in_dim(q, start, C_QBLOCK, axis=1)
        qchunk = (start + jnp.arange(C_QBLOCK)) // CHUNK
        mask = kchunk[None, :] <= qchunk[:, None]
        return attend(qb, k, v, scale, None, mask)

    out = lax.map(one_block, jnp.arange(s // C_QBLOCK))
    return jnp.moveaxis(out, 0, 1).reshape((n, s) + out.shape[3:])


def gated_out(o, g, w_out):
    return (o.reshape(g.shape) * jax.nn.silu(g)) @ w_out


def mixer_a(xp, xs, ck, cv, norm_g, w_in, q_g, k_g, rel, w_out):
    scale = A_HEAD_DIM ** -0.5

    def bias_fn(d):
        idx = jnp.clip(d, -A_REL_CLIP, A_REL_CLIP) + A_REL_CLIP
        return rel.astype(jnp.float32)[:, idx][:, None]

    def proj(x):
        n, s = x.shape[0], x.shape[1]
        z = rmsnorm(x, norm_g) @ w_in
        q, k, v, g = jnp.split(z, 4, axis=-1)
        q = rmsnorm(q.reshape(n, s, A_HEADS, 1, A_HEAD_DIM), q_g)
        k = rmsnorm(k.reshape(n, s, A_HEADS, A_HEAD_DIM), k_g)
        v = v.reshape(n, s, A_HEADS, A_HEAD_DIM)
        return q, k, v, g

    qp, kp, vp, gp = proj(xp)
    op = band_attend_prompt(qp, kp, vp, A_BAND_CHUNKS, bias_fn, scale)
    qs, ks, vs, gs = proj(xs)
    os_, kall, vall = band_attend_sample(qs, ck, cv, ks, vs, bias_fn, scale)
    return (gated_out(op, gp, w_out), gated_out(os_, gs, w_out),
            kp[:, -A_REACH:], vp[:, -A_REACH:], kall[:, -A_REACH:], vall[:, -A_REACH:])


def mixer_b(xp, xs, ck, cv, norm_g, w_in, q_g, k_g, sinks, w_out):
    scale = B_HEAD_DIM ** -0.5
    slopes = alibi_slopes(B_HEADS)[:, None, None]
    snk = sinks.reshape(B_KV_HEADS, B_GROUP)

    def bias_fn(d):
        return (-slopes * jnp.abs(d).astype(jnp.float32)).reshape(B_KV_HEADS, B_GROUP, d.shape[0], d.shape[1])

    def proj(x):
        n, s = x.shape[0], x.shape[1]
        z = rmsnorm(x, norm_g) @ w_in
        o1 = B_WIDTH
        o2 = o1 + B_KV_WIDTH
        o3 = o2 + B_KV_WIDTH
        q = rmsnorm(z[..., :o1].reshape(n, s, B_KV_HEADS, B_GROUP, B_HEAD_DIM), q_g)
        k = rmsnorm(z[..., o1:o2].reshape(n, s, B_KV_HEADS, B_HEAD_DIM), k_g)
        v = z[..., o2:o3].reshape(n, s, B_KV_HEADS, B_HEAD_DIM)
        return q, k, v, z[..., o3:]

    qp, kp, vp, gp = proj(xp)
    op = band_attend_prompt(qp, kp, vp, B_BAND_CHUNKS, bias_fn, scale, snk)
    qs, ks, vs, gs = proj(xs)
    os_, kall, vall = band_attend_sample(qs, ck, cv, ks, vs, bias_fn, scale, snk)
    return (gated_out(op, gp, w_out), gated_out(os_, gs, w_out),
            kp[:, -B_REACH:], vp[:, -B_REACH:], kall[:, -B_REACH:], vall[:, -B_REACH:])


def mixer_c(xp, xs, ckv_cache, ckr_cache, norm_g, w_in, qa_g, w_qb, kva_g, w_kvb, q_g, k_g, w_out):
    scale = C_QK ** -0.5

    def proj(x, pos):
        n, s = x.shape[0], x.shape[1]
        z = rmsnorm(x, norm_g) @ w_in
        o1 = C_Q_LORA
        o2 = o1 + C_KV_LORA
        o3 = o2 + C_ROPE
        q = (rmsnorm(z[..., :o1], qa_g) @ w_qb).reshape(n, s, C_HEADS, 1, C_QK)
        q = jnp.concatenate([rmsnorm(q[..., :C_NOPE], q_g[:C_NOPE]),
                             rope(rmsnorm(q[..., C_NOPE:], q_g[C_NOPE:]), pos)], axis=-1)
        ckv = rmsnorm(z[..., o1:o2], kva_g)
        kr = rope(rmsnorm(z[..., o2:o3], k_g[C_NOPE:]), pos)
        return q, ckv, kr, z[..., o3:]

    def expand(ckv, kr):
        n, s = ckv.shape[0], ckv.shape[1]
        kv = (ckv @ w_kvb).reshape(n, s, C_HEADS, C_NOPE + C_V)
        k = jnp.concatenate([rmsnorm(kv[..., :C_NOPE], k_g[:C_NOPE]),
                             jnp.broadcast_to(kr[:, :, None, :], (n, s, C_HEADS, C_ROPE))], axis=-1)
        return k, kv[..., C_NOPE:]

    qp, ckv_p, kr_p, gp = proj(xp, jnp.arange(xp.shape[1]))
    kp, vp = expand(ckv_p, kr_p)
    op = chunk_causal_attend_prompt(qp, kp, vp, scale)
    past = ckv_cache.shape[1]
    qs, ckv_s, kr_s, gs = proj(xs, past + jnp.arange(xs.shape[1]))
    ks, vs = expand(jnp.concatenate([ckv_cache, ckv_s], axis=1), jnp.concatenate([ckr_cache, kr_s], axis=1))
    os_ = attend(qs, ks, vs, scale)
    return (gated_out(op, gp, w_out), gated_out(os_, gs, w_out), ckv_p, kr_p, ckv_s, kr_s)


def setup_inputs(seed: int = 0) -> dict:
    key = jax.random.key(seed)
    keys = iter(jax.random.split(key, 40))

    def nrm(shape, scale=1.0):
        return jax.random.normal(next(keys), shape, jnp.float32) * scale

    def gain(shape):
        return 1.0 + 0.02 * nrm(shape)

    la = min(A_REACH, PAST_LEN)
    lb = min(B_REACH, PAST_LEN)
    return {
        'x_prompt': nrm((BATCH, SEQ, D_MODEL)),
        'x_sample': nrm((DEC_BATCH, DEC_SEQ, D_MODEL)),
        'cache_a_k': nrm((N_A, DEC_BATCH, la, A_HEADS, A_HEAD_DIM)),
        'cache_a_v': nrm((N_A, DEC_BATCH, la, A_HEADS, A_HEAD_DIM)),
        'cache_b_k': nrm((N_B, DEC_BATCH, lb, B_KV_HEADS, B_HEAD_DIM)),
        'cache_b_v': nrm((N_B, DEC_BATCH, lb, B_KV_HEADS, B_HEAD_DIM)),
        'cache_c_kv': nrm((N_C, DEC_BATCH, PAST_LEN, C_KV_LORA)),
        'cache_c_kr': nrm((N_C, DEC_BATCH, PAST_LEN, C_ROPE)),
        'a_norm': gain((N_A, D_MODEL)),
        'a_w_in': nrm((N_A, D_MODEL, 4 * A_WIDTH), D_MODEL ** -0.5),
        'a_q_norm': gain((N_A, A_HEAD_DIM)),
        'a_k_norm': gain((N_A, A_HEAD_DIM)),
        'a_rel_bias': nrm((N_A, A_HEADS, 2 * A_REL_CLIP + 1), 0.5),
        'a_w_out': nrm((N_A, A_WIDTH, D_MODEL), A_WIDTH ** -0.5),
        'b_norm': gain((N_B, D_MODEL)),
        'b_w_in': nrm((N_B, D_MODEL, 2 * B_WIDTH + 2 * B_KV_WIDTH), D_MODEL ** -0.5),
        'b_q_norm': gain((N_B, B_HEAD_DIM)),
        'b_k_norm': gain((N_B, B_HEAD_DIM)),
        'b_sinks': nrm((N_B, B_HEADS), 0.5),
        'b_w_out': nrm((N_B, B_WIDTH, D_MODEL), B_WIDTH ** -0.5),
        'c_norm': gain((N_C, D_MODEL)),
        'c_w_in': nrm((N_C, D_MODEL, C_Q_LORA + C_KV_LORA + C_ROPE + C_WIDTH), D_MODEL ** -0.5),
        'c_q_a_norm': gain((N_C, C_Q_LORA)),
        'c_w_qb': nrm((N_C, C_Q_LORA, C_HEADS * C_QK), C_Q_LORA ** -0.5),
        'c_kv_a_norm': gain((N_C, C_KV_LORA)),
        'c_w_kvb': nrm((N_C, C_KV_LORA, C_HEADS * (C_NOPE + C_V)), C_KV_LORA ** -0.5),
        'c_q_norm': gain((N_C, C_QK)),
        'c_k_norm': gain((N_C, C_QK)),
        'c_w_out': nrm((N_C, C_WIDTH, D_MODEL), C_WIDTH ** -0.5),
    }


def reference(x_prompt, x_sample, cache_a_k, cache_a_v, cache_b_k, cache_b_v, cache_c_kv, cache_c_kr,
              a_norm, a_w_in, a_q_norm, a_k_norm, a_rel_bias, a_w_out,
              b_norm, b_w_in, b_q_norm, b_k_norm, b_sinks, b_w_out,
              c_norm, c_w_in, c_q_a_norm, c_w_qb, c_kv_a_norm, c_w_kvb, c_q_norm, c_k_norm, c_w_out):
    xp, xs = x_prompt, x_sample
    ak_p, av_p, ak_s, av_s = [], [], [], []
    bk_p, bv_p, bk_s, bv_s = [], [], [], []
    ckv_p, ckr_p, ckv_s, ckr_s = [], [], [], []
    for i in range(DEPTH):
        j = i // N_MIXERS
        kind = i % N_MIXERS
        if kind == 0:
            dp, ds, s0, s1, s2, s3 = mixer_a(xp, xs, cache_a_k[j], cache_a_v[j], a_norm[j], a_w_in[j],
                                             a_q_norm[j], a_k_norm[j], a_rel_bias[j], a_w_out[j])
            ak_p.append(s0); av_p.append(s1); ak_s.append(s2); av_s.append(s3)
        elif kind == 1:
            dp, ds, s0, s1, s2, s3 = mixer_b(xp, xs, cache_b_k[j], cache_b_v[j], b_norm[j], b_w_in[j],
                                             b_q_norm[j], b_k_norm[j], b_sinks[j], b_w_out[j])
            bk_p.append(s0); bv_p.append(s1); bk_s.append(s2); bv_s.append(s3)
        else:
            dp, ds, s0, s1, s2, s3 = mixer_c(xp, xs, cache_c_kv[j], cache_c_kr[j], c_norm[j], c_w_in[j],
                                             c_q_a_norm[j], c_w_qb[j], c_kv_a_norm[j], c_w_kvb[j],
                                             c_q_norm[j], c_k_norm[j], c_w_out[j])
            ckv_p.append(s0); ckr_p.append(s1); ckv_s.append(s2); ckr_s.append(s3)
        xp = xp + dp
        xs = xs + ds
    return (xp, xs,
            jnp.stack(ak_p), jnp.stack(av_p), jnp.stack(bk_p), jnp.stack(bv_p), jnp.stack(ckv_p), jnp.stack(ckr_p),
            jnp.stack(ak_s), jnp.stack(av_s), jnp.stack(bk_s), jnp.stack(bv_s), jnp.stack(ckv_s), jnp.stack(ckr_s))
```

```python
import contextlib
import numpy as np
import concourse.bass as bass
import concourse.mybir as mybir
from concourse.bass_utils import run_bass_kernel_spmd

F32 = mybir.dt.float32
BF16 = mybir.dt.bfloat16
AF = mybir.ActivationFunctionType
ALU = mybir.AluOpType

NCORES = 8
DBG = set()
MASK_ENG = "pool"
D = 1024
SEQ = 4096
TB = 512
NPB = SEQ // TB
NS = 4
DS = 64
PAST = 2048
EPS = 1e-6
ROPE_THETA = 10000.0
LAYER_KINDS = ("A", "B", "C", "A")


class Buf:
    __slots__ = ("name", "w", "r", "wsem", "rsem", "dram", "excl")

    def __init__(self, name, dram=False, excl=False):
        self.name = name
        self.dram = dram
        self.excl = excl
        self.w = {}
        self.r = {}
        self.wsem = None
        self.rsem = None


class Sched:
    ENGS = ("pe", "act", "dve", "pool", "sp")

    def __init__(self, nc, stack):
        self.nc = nc
        self.stack = stack
        self.prog = {e: [] for e in self.ENGS}
        self.cnt = {e: 0 for e in self.ENGS}
        self.pending = {e: False for e in self.ENGS}
        self.seen = {e: {} for e in self.ENGS}
        self.sems = {}
        self.dcnt = {}
        self.final_tokens = {}
        for e in self.ENGS:
            self.sem("E_" + e)

    def sem(self, key):
        if key not in self.sems:
            self.sems[key] = self.stack.enter_context(self.nc.semaphore("s%d" % len(self.sems)))
        return self.sems[key]

    def _deps(self, eng, reads, writes):
        own = "E_" + eng
        m = {}
        for b in reads:
            for k, v in b.w.items():
                if k == own and eng == "pe":
                    continue
                if v > m.get(k, 0):
                    m[k] = v
            if b.excl:
                for k, v in b.r.items():
                    if k == own:
                        continue
                    if v > m.get(k, 0):
                        m[k] = v
        for b in writes:
            for src in (b.w, b.r):
                for k, v in src.items():
                    if k == own and eng == "pe":
                        continue
                    if v > m.get(k, 0):
                        m[k] = v
        out = []
        seen = self.seen[eng]
        for k, v in m.items():
            if seen.get(k, 0) >= v:
                continue
            seen[k] = v
            out.append((k, v))
        return out

    def op(self, eng, fn, reads=(), writes=(), inc=True):
        waits = self._deps(eng, reads, writes)
        key = "E_" + eng
        if inc:
            self.cnt[eng] += 1
            idx = self.cnt[eng]
            self.pending[eng] = False
            self.prog[eng].append((waits, fn, (key, 1)))
        else:
            idx = self.cnt[eng] + 1
            self.pending[eng] = True
            self.prog[eng].append((waits, fn, None))
        for b in reads:
            if b.r.get(key, 0) < idx:
                b.r[key] = idx
        for b in writes:
            if b.w.get(key, 0) < idx:
                b.w[key] = idx

    def dma(self, q, out_ap, in_ap, reads=(), writes=(), semkey=None, final=False):
        waits = self._deps(q, reads, writes)
        if semkey is None:
            sw = [b for b in writes if not b.dram]
            sr = [b for b in reads if not b.dram]
            qk = "sw" if q == "pool" else "hw"
            if sw:
                semkey = "W_%s_%s" % (sw[0].name, qk)
            else:
                semkey = "R_%s_%s" % (sr[0].name, qk)
        self.sem(semkey)
        self.dcnt[semkey] = self.dcnt.get(semkey, 0) + 16
        val = self.dcnt[semkey]

        def fn(e, out_ap=out_ap, in_ap=in_ap):
            return e.dma_start(out=out_ap, in_=in_ap)

        self.prog[q].append((waits, fn, (semkey, 16)))
        for b in reads:
            if b.r.get(semkey, 0) < val:
                b.r[semkey] = val
        for b in writes:
            if b.w.get(semkey, 0) < val:
                b.w[semkey] = val
        if final:
            self.final_tokens[semkey] = val

    def emit(self, block):
        engmap = {"pe": block.tensor, "act": block.scalar, "dve": block.vector, "pool": block.gpsimd,
                  "sp": block.sync}
        for e in self.ENGS:
            assert not self.pending[e], e
        fw = [(k, v) for k, v in self.final_tokens.items()]
        sems = self.sems
        for e in self.ENGS:
            prog = self.prog[e]

            def body(eng, prog=prog, e=e):
                for waits, fn, upd in prog:
                    for k, v in waits[:-1]:
                        eng.wait_ge(sems[k], v)
                    ins = fn(eng)
                    if waits:
                        ins._wait_ge(sems[waits[-1][0]], v if False else waits[-1][1])
                    if upd is not None:
                        ins.then_inc(sems[upd[0]], upd[1])
                if e == "sp":
                    for k, v in fw:
                        eng.wait_ge(sems[k], v)

            engmap[e](body)


class Pipe:
    def __init__(self, depth=1):
        self.q = []
        self.depth = depth

    def push(self, fn):
        self.q.append(fn)
        while len(self.q) > self.depth:
            self.q.pop(0)()

    def flush(self):
        while self.q:
            self.q.pop(0)()


class Rot:
    def __init__(self, items):
        self.items = items
        self.i = 0

    def next(self):
        it = self.items[self.i % len(self.items)]
        self.i += 1
        return it


GP_NCOL = 58


class Builder:
    def __init__(self, layers=(0, 1, 2, 3), npb=NPB, do_sample=True):
        self.layers = tuple(layers)
        self.npb = npb
        self.do_sample = do_sample
        self.nc = bass.Bass("TRN2", target_bir_lowering=False)
        self.st = contextlib.ExitStack()
        self.S = Sched(self.nc, self.st)
        self.in_names = []
        self.out_names = []

    def din(self, name, shape, dt=F32):
        self.in_names.append(name)
        return self.nc.dram_tensor(name, list(shape), dt, kind="ExternalInput").ap()

    def dout(self, name, shape):
        self.out_names.append(name)
        return self.nc.dram_tensor(name, list(shape), F32, kind="ExternalOutput").ap()

    def dscratch(self, name, shape, dt):
        return self.nc.dram_tensor(name, list(shape), dt).ap()

    def sb(self, name, shape, dt):
        return self.st.enter_context(self.nc.sbuf_tensor(name, list(shape), dt))

    def ps(self, name, shape, dt=F32):
        return self.st.enter_context(self.nc.psum_tensor(name, list(shape), dt))

    def rot_sb(self, name, n, shape, dt):
        return Rot([(self.sb("%s%d" % (name, i), shape, dt), Buf("%s%d" % (name, i))) for i in range(n)])

    def mark(self, label):
        if not hasattr(self, "marks"):
            self.marks = []
        self.marks.append((label, len(self.S.prog["pe"])))

    def mm(self, out, lhsT, rhs, start, stop, reads, writes, inc=True):
        self.S.op("pe", lambda e: e.matmul(out, lhsT, rhs, start=start, stop=stop), reads, writes, inc=inc)

    def tr(self, out, in_, ident, reads, writes, inc=True):
        self.S.op("pe", lambda e: e.transpose(out, in_, ident), reads, writes, inc=inc)

    def act(self, out, in_, func, reads, writes, scale=1.0, bias=0.0):
        self.S.op("act", lambda e: e.activation(out=out, in_=in_, func=func, bias=bias, scale=scale), reads, writes)

    def copy(self, eng, out, in_, reads, writes):
        if eng == "act":
            self.S.op("act", lambda e: e.copy(out, in_), reads, writes)
        else:
            self.S.op(eng, lambda e: e.tensor_copy(out, in_), reads, writes)

    def tt(self, eng, out, in0, in1, op, reads, writes):
        self.S.op(eng, lambda e: e.tensor_tensor(out=out, in0=in0, in1=in1, op=op), reads, writes)

    def stt(self, eng, out, in0, scalar, in1, op0, op1, reads, writes):
        self.S.op(eng, lambda e: e.scalar_tensor_tensor(out=out, in0=in0, scalar=scalar, in1=in1, op0=op0, op1=op1),
                  reads, writes)

    def tsadd(self, eng, out, in0, s1, reads, writes):
        self.S.op(eng, lambda e: e.tensor_scalar_add(out, in0, s1), reads, writes)

    def recip(self, out, in_, reads, writes):
        self.S.op("dve", lambda e: e.reciprocal(out, in_), reads, writes)

    def memset(self, eng, ap, val, writes):
        self.S.op(eng, lambda e: e.memset(ap, val), (), writes)

    def build(self):
        nc, S = self.nc, self.S
        self.xp = self.din("xp", [SEQ, D])
        self.xs = self.din("xs", [NS * DS, D])
        self.cak = self.din("cak", [2, NS, 512, 1024])
        self.cav = self.din("cav", [2, NS, 512, 1024])
        self.cbk = self.din("cbk", [NS, 128, 256])
        self.cbv = self.din("cbv", [NS, 128, 256])
        self.cckv = self.din("cckv", [NS, PAST, 256])
        self.cckr = self.din("cckr", [NS, PAST, 32])
        self.a_w_in = self.din("a_w_in", [2, D, 4096])
        self.a_w_out = self.din("a_w_out", [2, D, D])
        self.b_w_in = self.din("b_w_in", [D, 2560])
        self.b_w_out = self.din("b_w_out", [D, D])
        self.c_w_in = self.din("c_w_in", [D, 1824])
        self.c_w_qb = self.din("c_w_qb", [512, 1536])
        self.c_w_kvb = self.din("c_w_kvb", [256, 2048])
        self.c_w_out = self.din("c_w_out", [D, D])
        self.gpack_d = self.din("gpack", [128, GP_NCOL])
        self.relT = self.din("relT", [2, 8, 128, 640])
        self.cpack_d = self.din("cpack", [128, 8])
        self.ident_d = self.din("ident", [128, 128])
        self.bd_d = self.din("bdpack", [128, 4, 128])
        self.distb_d = self.din("distb", [128, 256])
        self.maskc_d = self.din("maskc", [128, 512])
        self.csP = self.din("csP", [2, 32, SEQ])
        self.csS = self.din("csS", [2, 32, NS * DS])

        self.y_p = self.dout("y_p", [SEQ, D])
        self.y_s = self.dout("y_s", [NS * DS, D])
        self.ak_p = self.dout("ak_p", [2, 512, 1024])
        self.av_p = self.dout("av_p", [2, 512, 1024])
        self.bk_p = self.dout("bk_p", [128, 256])
        self.bv_p = self.dout("bv_p", [128, 256])
        self.ckv_p = self.dout("ckv_p", [SEQ, 256])
        self.ckr_p = self.dout("ckr_p", [SEQ, 32])
        self.ak_s = self.dout("ak_s", [2, NS, 512, 1024])
        self.av_s = self.dout("av_s", [2, NS, 512, 1024])
        self.bk_s = self.dout("bk_s", [NS, 128, 256])
        self.bv_s = self.dout("bv_s", [NS, 128, 256])
        self.ckv_s = self.dout("ckv_s", [NS * DS, 256])
        self.ckr_s = self.dout("ckr_s", [NS * DS, 32])

        self.KCp = self.dscratch("KCp", [16, 96, SEQ], BF16)
        self.VCp = self.dscratch("VCp", [16, NPB, 128, 256], BF16)
        self.KCs = self.dscratch("KCs", [NS, 16, 96, 2560], BF16)
        self.VCs = self.dscratch("VCs", [NS, 16, 5, 128, 256], BF16)
        self.EBA_d = self.dscratch("EBA", [2, 8, 128, 640], BF16)
        self.EBB_d = self.dscratch("EBBd", [16, 128, 256], BF16)
        self.B_KCp = [Buf("KCp%d" % g, True) for g in range(NPB)]
        self.B_VCp = [Buf("VCp%d" % g, True) for g in range(NPB)]
        self.B_KCs = [[Buf("KCs%d_%d" % (s, g), True) for g in range(5)] for s in range(NS)]
        self.B_VCs = [[Buf("VCs%d_%d" % (s, g), True) for g in range(5)] for s in range(NS)]
        self.B_EBA = Buf("EBAd", True)
        self.B_EBB = Buf("EBBd", True)

        self.xT = self.sb("xT", [128, 8, TB], F32); self.B_xT = Buf("xT")
        self.xn = self.sb("xn", [128, 8, TB], BF16); self.B_xn = Buf("xn")
        self.gT = self.sb("gT", [128, 8, TB], BF16); self.B_gT = Buf("gT")
        self.QTt = self.sb("QT", [128, 16 * TB], BF16); self.B_QT = Buf("QT")
        self.QT = self.QTt[:, :].rearrange("p (h t) -> p h t", h=16)
        self.big = self.QTt[:, :].bitcast(F32).rearrange("p (c t) -> p c t", c=8)
        self.AK = [self.sb("AK%d" % j, [128, 8, 2 * TB], BF16) for j in range(2)]
        self.AV = [self.sb("AV%d" % j, [128, 8, 1024], BF16) for j in range(2)]
        self.B_AK = [[Buf("AK%d_%d" % (j, s)) for s in range(2)] for j in range(2)]
        self.B_AV = [[Buf("AV%d_%d" % (j, s)) for s in range(2)] for j in range(2)]
        self.BK = self.sb("BK", [128, 4, 2 * TB], BF16); self.B_BK = [Buf("BK0"), Buf("BK1")]
        self.BV = self.sb("BV", [128, 8, 256], BF16); self.B_BV = [Buf("BV0"), Buf("BV1")]
        self.slabs = self.rot_sb("slab", 2, [128, 4096], BF16)
        self.wsw = self.sb("wsw", [128, 4, 16, 32], BF16); self.B_wsw = Buf("wsw")
        self.wsk = self.sb("wsk", [128, 8, 32], BF16); self.B_wsk = Buf("wsk")
        self.gp = self.sb("gp", [128, GP_NCOL], F32); self.B_gp = Buf("gp")
        self.esink = self.sb("esink", [128, 8], F32); self.B_esink = Buf("esink")
        self.cpk = self.sb("cpk", [128, 8], F32); self.B_cpk = Buf("cpk")
        self.ident = self.sb("ident_sb", [128, 128], F32); self.B_ident = Buf("ident")
        self.identb = self.sb("identb", [128, 128], BF16); self.B_identb = Buf("identb")
        self.bd = self.sb("bd", [128, 4, 128], BF16); self.B_bd = Buf("bd")
        self.zer = self.sb("zer", [128, TB], BF16); self.B_zer = Buf("zer")
        self.maskc = self.sb("maskc_sb", [128, TB], BF16); self.B_maskc = Buf("maskc")
        self.cs = self.sb("cs", [128, 2, TB], F32); self.B_cs = Buf("cs")
        self.eba = self.rot_sb("eba", 2, [128, 640], BF16)
        self.sq = self.rot_sb("sq", 2, [128, TB], BF16)
        self.f32a = self.rot_sb("fa", 2, [128, TB], F32)
        self.f32b = self.rot_sb("fb", 2, [128, TB], F32)
        self.Eb = self.rot_sb("E", 4, [128, TB], BF16)
        self.stage = self.rot_sb("stg", 2, [128, D], F32)
        self.po32 = self.rot_sb("po32", 2, [128, TB], F32)
        self.kst32 = Rot([self.f32b.items[0]])
        self.qan = self.sb("qan", [128, 4, TB], BF16); self.B_qan = Buf("qan")
        self.ckvT = self.sb("ckvT", [128, 2, TB], BF16); self.B_ckvT = Buf("ckvT")
        self.ckc = self.sb("ckc", [128, 2, TB], BF16); self.B_ckc = Buf("ckc")
        self.krb = self.sb("krb", [128, TB], BF16); self.B_krb = Buf("krb")
        self.krc = self.sb("krc", [128, TB], BF16); self.B_krc = Buf("krc")
        self.kts = self.rot_sb("kts", 2, [128, TB], BF16)
        self.sc4 = self.rot_sb("sc4", 1, [128, 2048], BF16)
        self.kcg = self.rot_sb("kcg", 3, [128, TB], BF16)
        self.vcg = self.rot_sb("vcg", 3, [128, 4, 192], BF16)
        self.cst = self.rot_sb("cst", 1, [128, 4, 256], BF16)
        self.cst2 = self.rot_sb("cst2", 1, [128, 4, 32], BF16)
        self.pz = Rot([(self.ps("pz%d" % i, [128, TB]), Buf("pz%d" % i, excl=True)) for i in range(3)])
        self.pss = Rot([(self.ps("pss0", [128, TB]), Buf("pss0", excl=True))])
        self.pst = Rot([(self.ps("pst%d" % i, [128, TB]), Buf("pst%d" % i, excl=True)) for i in range(2)])
        self.pst3 = Rot(self.pst.items + [self.pz.items[2]])
        self.pzg = Rot(self.pz.items[0:2])
        self.po = self.ps("po", [128, TB]); self.B_po = Buf("po", excl=True)
        self.pd = self.ps("pd", [128, TB]); self.B_pd = Buf("pd", excl=True)

        self.setup_consts()
        self.plan_weights()
        blocks = [("p", i) for i in range(self.npb)]
        if self.do_sample:
            blocks.append(("s", 0))
        for kind, i in blocks:
            self.run_block(kind, i)
        assert self.w_used == len(self.wplan), (self.w_used, len(self.wplan))
        with nc.Block() as block:
            S.emit(block)
        self.st.close()
        return nc

    def setup_consts(self):
        S = self.S
        ck = "const"
        S.dma("sp", self.gp[:], self.gpack_d[:, :], writes=[self.B_gp], semkey=ck)
        S.dma("sp", self.cpk[:], self.cpack_d[:, :], writes=[self.B_cpk], semkey=ck)
        S.dma("sp", self.ident[:], self.ident_d[:, :], writes=[self.B_ident], semkey=ck)
        S.dma("pool", self.identb[:], self.ident_d[:, :], writes=[self.B_identb], semkey="constp")
        S.dma("pool", self.bd[:], self.bd_d[:, :, :], writes=[self.B_bd], semkey="constp")
        S.dma("pool", self.maskc[:], self.maskc_d[:, :], writes=[self.B_maskc], semkey="constp")
        tot = S.dcnt[ck]
        for b in (self.B_gp, self.B_cpk, self.B_ident):
            b.w[ck] = tot
        tot = S.dcnt["constp"]
        for b in (self.B_identb, self.B_bd, self.B_maskc):
            b.w["constp"] = tot
        self.memset("dve", self.zer[:], 0.0, [self.B_zer])
        for vt, B_vt in self.vcg.items:
            self.memset("dve", vt[:, :, :], 1.0, [B_vt])
        self.act(self.esink[:], self.gp[:, 48:56], AF.Exp, [self.B_gp], [self.B_esink])
        if 1 in self.layers:
            dist, B_dist = self.f32a.next()
            S.dma("sp", dist[:, 0:256], self.distb_d[:, :], writes=[B_dist])
            for h in range(16):
                slope = float(2.0 ** (-8.0 * (h + 1) / 16))
                e, B_e = self.eba.next()
                self.act(e[:, 0:256], dist[:, 0:256], AF.Exp, [B_dist], [B_e], scale=-slope)
                self.memset("dve", e[0:64, 192:256], 0.0, [B_e])
                self.memset("dve", e[64:128, 0:64], 0.0, [B_e])
                S.dma("sp", self.EBB_d[h, :, :], e[:, 0:256], reads=[B_e], writes=[self.B_EBB])
        for j in range(2):
            if (3 * j) not in self.layers:
                continue
            for h in range(8):
                t32, B_t32 = self.stage.next()
                S.dma("sp", t32[:, 0:640], self.relT[j, h, :, :], writes=[B_t32])
                e, B_e = self.eba.next()
                self.act(e[:, :], t32[:, 0:640], AF.Exp, [B_t32], [B_e])
                self.memset("dve", e[0:64, 576:640], 0.0, [B_e])
                self.memset("dve", e[64:128, 0:64], 0.0, [B_e])
                S.dma("sp", self.EBA_d[j, h, :, :], e[:, :], reads=[B_e], writes=[self.B_EBA])

    def layer_slabs(self, li):
        kind = LAYER_KINDS[li]
        j = li // 3
        out = []
        if kind == "A":
            w = self.a_w_in[j]
            for c0 in range(0, 3072, 512):
                out.append(("A%d_in%d" % (j, c0), w[:, c0:c0 + 512], 8, 512))
            for c0 in (3072, 3584):
                out.append(("A%d_g%d" % (j, c0), w[:, c0:c0 + 512], 8, 512))
            for c0 in (0, 512):
                out.append(("A%d_o%d" % (j, c0), self.a_w_out[j][:, c0:c0 + 512], 8, 512))
        elif kind == "B":
            w = self.b_w_in
            for c0 in (0, 512, 1024):
                out.append(("B_in%d" % c0, w[:, c0:c0 + 512], 8, 512))
            for c0 in (1536, 2048):
                out.append(("B_g%d" % c0, w[:, c0:c0 + 512], 8, 512))
            for c0 in (0, 512):
                out.append(("B_o%d" % c0, self.b_w_out[:, c0:c0 + 512], 8, 512))
        else:
            w = self.c_w_in
            out.append(("C_qa", w[:, 0:512], 8, 512))
            out.append(("C_kvr", w[:, 512:800], 8, 288))
            out.append(("C_kvb", self.c_w_kvb[:, :], 2, 2048))
            out.append(("C_qb0", self.c_w_qb[:, 0:960], 4, 960))
            out.append(("C_qb1", self.c_w_qb[:, 960:1536], 4, 576))
            for c0 in (800, 1312):
                out.append(("C_g%d" % c0, w[:, c0:c0 + 512], 8, 512))
            for c0 in (0, 512):
                out.append(("C_o%d" % c0, self.c_w_out[:, c0:c0 + 512], 8, 512))
        return out

    def plan_weights(self):
        nblk = self.npb + (1 if self.do_sample else 0)
        self.wplan = []
        for _ in range(nblk):
            for li in self.layers:
                self.wplan.extend(self.layer_slabs(li))
        self.w_issued = 0
        self.w_used = 0
        self.w_live = {}
        self.wscratch = {}

    def _issue_slab(self):
        name, src, nk, ncol = self.wplan[self.w_issued]
        t, B = self.slabs.next()
        view = t[:, 0:nk * ncol].rearrange("p (k c) -> p k c", k=nk)
        if name not in self.wscratch:
            self.S.dma("pool", view, src.rearrange("(k p) c -> p k c", p=128), writes=[B])
            ws = self.dscratch("ws_" + name, [128, nk * ncol], BF16)
            Bws = Buf("ws_" + name, True)
            self.wscratch[name] = (ws, Bws)
            self.S.dma("sp", ws[:, :], t[:, 0:nk * ncol], reads=[B], writes=[Bws])
        else:
            ws, Bws = self.wscratch[name]
            self.S.dma("pool", t[:, 0:nk * ncol], ws[:, :], reads=[Bws], writes=[B])
        self.w_live[self.w_issued] = (view, B)
        self.w_issued += 1

    def slab(self, name, ahead=1):
        idx = self.w_used
        assert self.wplan[idx][0] == name, (self.wplan[idx][0], name)
        while self.w_issued < min(len(self.wplan), idx + 1 + ahead):
            self._issue_slab()
        self.w_used += 1
        return self.w_live.pop(idx)

    def run_block(self, kind, i):
        S = self.S
        T = TB if kind == "p" else NS * DS
        self.kind, self.bi, self.T = kind, i, T
        src = self.xp if kind == "p" else self.xs
        row0 = i * TB if kind == "p" else 0
        for tb in range(T // 128):
            xs_, B_xs = self.stage.next()
            S.dma("sp", xs_[:, :], src[row0 + tb * 128: row0 + (tb + 1) * 128, :], writes=[B_xs])
            for half in range(2):
                pz, B_pz = self.pz.next()
                for c in range(4):
                    cc = half * 4 + c
                    self.tr(pz[:, c * 128:(c + 1) * 128], xs_[:, cc * 128:(cc + 1) * 128], self.ident[:, :],
                            [B_xs, self.B_ident], [B_pz], inc=(c == 3))
                self.copy("act" if half == 0 else "dve",
                          self.xT[:, half * 4:half * 4 + 4, tb * 128:(tb + 1) * 128],
                          pz[:, :].rearrange("p (c t) -> p c t", c=4), [B_pz], [self.B_xT])
        for li in self.layers:
            k = LAYER_KINDS[li]
            self.mark("%s%d L%d %s rms+proj" % (kind, i, li, k))
            if k == "A":
                self.layer_a(li // 3)
            elif k == "B":
                self.layer_b()
            else:
                self.layer_c()
        self.mark("%s%d store" % (kind, i))
        dst = self.y_p if kind == "p" else self.y_s
        for tb in range(T // 128):
            os_, B_os = self.stage.next()
            for half in range(2):
                pz, B_pz = self.pz.next()
                for c in range(4):
                    cc = half * 4 + c
                    self.tr(pz[:, c * 128:(c + 1) * 128], self.xT[:, cc, tb * 128:(tb + 1) * 128], self.ident[:, :],
                            [self.B_xT, self.B_ident], [B_pz], inc=(c == 3))
                self.copy("act" if half == 0 else "dve", os_[:, half * 512:(half + 1) * 512], pz[:, :],
                          [B_pz], [B_os])
            S.dma("sp", dst[row0 + tb * 128: row0 + (tb + 1) * 128, :], os_[:, :], reads=[B_os], final=True)

    def rstd_from(self, pss, B_pss, r0, rows, T, scale):
        r, B_r = self.f32a.next()
        rd = [B_pss] if not hasattr(scale, "shape") else [B_pss, self.B_cpk]
        self.act(r[r0:r0 + rows, 0:T], pss[r0:r0 + rows, 0:T], AF.Ln, rd, [B_r], scale=scale, bias=EPS)
        self.act(r[r0:r0 + rows, 0:T], r[r0:r0 + rows, 0:T], AF.Exp, [B_r], [B_r], scale=-0.5)
        return r, B_r

    def rms_xn(self, gcol0):
        T = self.T
        sqx, B_sqx = self.gT, self.B_gT
        self.act(sqx[:, :, 0:T], self.xT[:, :, 0:T], AF.Square, [self.B_xT], [B_sqx])
        pss, B_pss = self.pss.next()
        for c in range(8):
            self.mm(pss[:, 0:T], self.bd[:, 0, :], sqx[:, c, 0:T], c == 0, c == 7, [self.B_bd, B_sqx], [B_pss],
                    inc=(c == 7))
        r, B_r = self.rstd_from(pss, B_pss, 0, 128, T, 1.0 / D)
        for c in range(8):
            self.stt("dve", self.xn[:, c, 0:T], self.xT[:, c, 0:T], self.gp[:, gcol0 + c:gcol0 + c + 1], r[:, 0:T],
                     ALU.mult, ALU.mult, [self.B_xT, self.B_gp, B_r], [self.B_xn])

    def proj_fm(self, wv, B_w, col0, M, n, nk, rhs_fn, rhs_bufs, orow=0, pz_pair=None):
        pz, B_pz = self.pz.next() if pz_pair is None else pz_pair
        for k in range(nk):
            self.mm(pz[orow:orow + M, 0:n], wv[:, k, col0:col0 + M], rhs_fn(k), k == 0, k == nk - 1,
                    [B_w] + rhs_bufs, [B_pz], inc=(k == nk - 1))
        return pz, B_pz

    def sumsq_rstd(self, pz, B_pz, r0, rows, n, bd_idx, scale):
        sq, B_sq = self.sq.next()
        self.act(sq[r0:r0 + rows, 0:n], pz[r0:r0 + rows, 0:n], AF.Square, [B_pz], [B_sq])
        pss, B_pss = self.pss.next()
        self.mm(pss[r0:r0 + rows, 0:n], self.bd[r0:r0 + rows, bd_idx, r0:r0 + rows], sq[r0:r0 + rows, 0:n], True, True,
                [self.B_bd, B_sq], [B_pss])
        return self.rstd_from(pss, B_pss, r0, rows, n, scale)

    def attn_tile(self, pipe, kt, nk, q, N, scale, v, orow, M, q_lo, eb, rk, rv, reb, first, last,
                  zero=None, after=None, fused_acc=None):
        pst, B_pst = self.pst3.next()
        self.mm(pst[0:nk, 0:N], kt, q, True, True, rk + [self.B_QT], [B_pst])
        P, B_P = self.Eb.next()
        self.act(P[0:nk, 0:N], pst[0:nk, 0:N], AF.Exp, [B_pst], [B_P], scale=scale)
        if eb is not None:
            self.tt(MASK_ENG, P[0:nk, 0:N], P[0:nk, 0:N], eb, ALU.mult, [B_P] + reb, [B_P])

        def stage_b():
            if fused_acc is not None:
                acc, B_acc = fused_acc
                self.mm(acc[:, q_lo:q_lo + N], v, P[0:nk, 0:N], first, last, rv + [B_P], [B_acc])
            else:
                if zero is not None:
                    self.zero_acc(zero)
                self.mm(self.po[orow:orow + M, q_lo:q_lo + N], v, P[0:nk, 0:N], first, last, rv + [B_P],
                        [self.B_po], inc=False)
                self.mm(self.pd[orow:orow + M, q_lo:q_lo + N], self.bd[0:nk, 0, 0:M], P[0:nk, 0:N], first, last,
                        [self.B_bd, B_P], [self.B_pd])
            if after is not None:
                after()

        pipe.push(stage_b)

    def gate_post_c(self, G, chunk, n):
        pz, B_pz, t, B_t = G
        o32, B_o32 = self.po32.next()
        self.copy("dve", o32[0:64, 0:n], self.po[0:64, 0:n], [self.B_po], [B_o32])
        self.copy("dve", o32[64:128, 0:n], self.pd[64:128, 0:n], [self.B_pd], [B_o32])
        self.stt("dve", t[0:64, 0:n], t[0:64, 0:n], 1.0, self.po[64:128, 0:n], ALU.add, ALU.mult,
                 [B_t, self.B_po], [B_t])
        self.stt("dve", t[64:128, 0:n], t[64:128, 0:n], 1.0, self.pd[0:64, 0:n], ALU.add, ALU.mult,
                 [B_t, self.B_pd], [B_t])
        self.act(t[:, 0:n], t[:, 0:n], AF.Ln, [B_t], [B_t])
        self.act(t[:, 0:n], t[:, 0:n], AF.Exp, [B_t], [B_t], scale=-1.0)
        self.tt("dve", t[:, 0:n], t[:, 0:n], pz[:, 0:n], ALU.mult, [B_t, B_pz], [B_t])
        self.tt("dve", self.gT[:, chunk, 0:n], t[:, 0:n], o32[:, 0:n], ALU.mult, [B_t, B_o32], [self.B_gT])

    def zero_acc(self, T):
        self.mm(self.po[:, 0:T], self.zer[:, 0:128], self.zer[:, 0:T], True, False, [self.B_zer], [self.B_po],
                inc=False)
        self.mm(self.pd[:, 0:T], self.zer[:, 0:128], self.zer[:, 0:T], True, False, [self.B_zer], [self.B_pd])

    def gate_pre(self, gw, B_gw, gcol, c0, n):
        pz, B_pz = self.proj_fm(gw, B_gw, gcol, 128, n, 8, lambda k: self.xn[:, k, c0:c0 + n], [self.B_xn],
                                pz_pair=self.pzg.next())
        t, B_t = self.f32b.next()
        self.act(t[:, 0:n], pz[:, 0:n], AF.Exp, [B_pz], [B_t], scale=-1.0)
        return pz, B_pz, t, B_t

    def gate_post(self, G, chunk, c0, n, esink_ap=None):
        pz, B_pz, t, B_t = G
        o32, B_o32 = self.po32.next()
        self.copy("act" if esink_ap is None else "dve", o32[:, 0:n], self.po[:, c0:c0 + n], [self.B_po], [B_o32])
        if esink_ap is not None:
            d, B_d = self.f32a.next()
            self.tsadd("dve", d[:, 0:n], self.pd[:, c0:c0 + n], esink_ap, [self.B_pd, self.B_esink], [B_d])
            self.stt("dve", t[:, 0:n], t[:, 0:n], 1.0, d[:, 0:n], ALU.add, ALU.mult, [B_t, B_d], [B_t])
        else:
            self.stt("dve", t[:, 0:n], t[:, 0:n], 1.0, self.pd[:, c0:c0 + n], ALU.add, ALU.mult,
                     [B_t, self.B_pd], [B_t])
        self.act(t[:, 0:n], t[:, 0:n], AF.Ln, [B_t], [B_t])
        self.act(t[:, 0:n], t[:, 0:n], AF.Exp, [B_t], [B_t], scale=-1.0)
        self.tt("dve", t[:, 0:n], t[:, 0:n], pz[:, 0:n], ALU.mult, [B_t, B_pz], [B_t])
        self.tt("dve", self.gT[:, chunk, c0:c0 + n], t[:, 0:n], o32[:, 0:n], ALU.mult, [B_t, B_o32], [self.B_gT])

    def out_proj(self, names):
        T = self.T
        for si, name in enumerate(names):
            wv, B_w = self.slab(name)
            for oc in range(4):
                pz, B_pz = self.proj_fm(wv, B_w, oc * 128, 128, T, 8, lambda k: self.gT[:, k, 0:T], [self.B_gT])
                c = si * 4 + oc
                self.tt("dve", self.xT[:, c, 0:T], self.xT[:, c, 0:T], pz[:, 0:T], ALU.add, [self.B_xT, B_pz],
                        [self.B_xT])

    def pzb(self):
        pz, B = self.pz.next()
        return pz[:, :].bitcast(BF16), B

    def layer_a(self, j):
        S = self.S
        T, kind, bi = self.T, self.kind, self.bi
        cur = bi % 2 if kind == "p" else 0
        prev = 1 - cur
        want_out = (kind == "s") or (bi == NPB - 1)
        scale = 128.0 ** -0.5
        self.rms_xn(8 * j)
        AK, AV = self.AK[j], self.AV[j]
        B_AKc, B_AVc = self.B_AK[j][cur], self.B_AV[j][cur]
        B_AKp, B_AVp = self.B_AK[j][prev], self.B_AV[j][prev]
        xn_fn = lambda k: self.xn[:, k, 0:T]
        qkpipe = Pipe(1)
        for which in range(2):
            gcol = 32 + 2 * j + which
            for sl in range(2):
                wv, B_w = self.slab("A%d_in%d" % (j, which * 1024 + sl * 512))
                for hh in range(4):
                    h = sl * 4 + hh
                    pz, B_pz = self.proj_fm(wv, B_w, hh * 128, 128, T, 8, xn_fn, [self.B_xn])

                    def stage2(pz=pz, B_pz=B_pz, h=h, which=which, gcol=gcol):
                        r, B_r = self.sumsq_rstd(pz, B_pz, 0, 128, T, 0, 1.0 / 128)
                        if which == 0:
                            self.stt("dve", self.QT[:, h, 0:T], pz[:, 0:T], self.gp[:, gcol:gcol + 1], r[:, 0:T],
                                     ALU.mult, ALU.mult, [B_pz, self.B_gp, B_r], [self.B_QT])
                        elif not want_out:
                            self.stt("dve", AK[:, h, cur * TB:cur * TB + T], pz[:, 0:T], self.gp[:, gcol:gcol + 1],
                                     r[:, 0:T], ALU.mult, ALU.mult, [B_pz, self.B_gp, B_r], [B_AKc])
                        else:
                            k32, B_k32 = self.kst32.next()
                            self.stt("dve", k32[:, 0:T], pz[:, 0:T], self.gp[:, gcol:gcol + 1], r[:, 0:T],
                                     ALU.mult, ALU.mult, [B_pz, self.B_gp, B_r], [B_k32])
                            self.copy("act", AK[:, h, cur * TB:cur * TB + T], k32[:, 0:T], [B_k32], [B_AKc])
                            self.emit_k_out_a(j, h, k32, B_k32)

                    qkpipe.push(stage2)
        for sl in range(2):
            wv, B_w = self.slab("A%d_in%d" % (j, 2048 + sl * 512))
            if sl == 0:
                vfirst = True
            ntile = 4 if kind == "p" else NS
            rows = 128 if kind == "p" else 64
            for tb in range(ntile):
                pz, B_pz = self.pz.next()
                for k in range(8):
                    self.mm(pz[0:rows, :], self.xn[:, k, tb * rows:(tb + 1) * rows], wv[:, k, :], k == 0, k == 7,
                            [self.B_xn, B_w], [B_pz], inc=(k == 7))
                qkpipe.flush()
                self.copy("dve" if tb % 2 == 0 else "act", AV[0:rows, cur * 4 + tb, sl * 512:(sl + 1) * 512],
                          pz[0:rows, :], [B_pz], [B_AVc])
                if want_out:
                    os_, B_os = self.stage.next()
                    self.copy("dve", os_[0:rows, 0:512], pz[0:rows, :], [B_pz], [B_os])
                    if kind == "p":
                        dst = self.av_p[j, tb * 128:(tb + 1) * 128, sl * 512:(sl + 1) * 512]
                    else:
                        dst = self.av_s[j, tb, 448:512, sl * 512:(sl + 1) * 512]
                    S.dma("sp", dst, os_[0:rows, 0:512], reads=[B_os], final=True)
        self.mark("  A attn")
        pipe = Pipe(3)

        def load_eb(h):
            eb, B_eb = self.eba.next()
            S.dma("sp", eb[:, :], self.EBA_d[j, h, :, :], reads=[self.B_EBA], writes=[B_eb])
            return eb, B_eb

        if kind == "p":
            nxt_eb = load_eb(0)
            for h in range(8):
                if h % 4 == 0:
                    gw, B_gw = self.slab("A%d_g%d" % (j, 3072 + (h // 4) * 512))
                eb, B_eb = nxt_eb
                if h < 7:
                    nxt_eb = load_eb(h + 1)
                G = self.gate_pre(gw, B_gw, (h % 4) * 128, 0, T)
                tiles = [("c", 0)]
                if bi > 0:
                    tiles += [("p", a) for a in range(4)]
                tiles += [("c", a) for a in range(1, 4)]
                for ti, (w_, a) in enumerate(tiles):
                    if w_ == "c":
                        N, q_lo, r0, slot, Bk, Bv = TB - 128 * a, 128 * a, 0, cur, B_AKc, B_AVc
                    else:
                        N, q_lo, r0, slot, Bk, Bv = 128 * (a + 1), 0, TB - 128 * a, prev, B_AKp, B_AVp
                    lastt = ti == len(tiles) - 1
                    after = (lambda G=G, h=h: self.gate_post(G, h, 0, T)) if lastt else None
                    self.attn_tile(pipe, AK[:, h, slot * TB + a * 128: slot * TB + (a + 1) * 128], 128,
                                   self.QT[:, h, q_lo:q_lo + N], N, scale,
                                   AV[:, slot * 4 + a, h * 128:(h + 1) * 128], 0, 128, q_lo,
                                   eb[:, r0:r0 + N], [Bk], [Bv], [B_eb], ti == 0, lastt, after=after)
            pipe.flush()
        else:
            gws = [None, None]
            for s in range(NS):
                rj = j if s % 2 == 0 else 1 - j
                CK, CV = self.AK[rj], self.AV[rj]
                B_CK, B_CV = self.B_AK[rj][1], self.B_AV[rj][1]
                self.load_cache_a(j, s, CK, CV, B_CK, B_CV)
                nxt_eb = load_eb(0)
                for h in range(8):
                    if s == 0 and h % 4 == 0:
                        gws[h // 4] = self.slab("A%d_g%d" % (j, 3072 + (h // 4) * 512), ahead=1 - h // 4)
                    eb, B_eb = nxt_eb
                    if h < 7:
                        nxt_eb = load_eb(h + 1)
                    gw, B_gw = gws[h // 4]
                    G = self.gate_pre(gw, B_gw, (h % 4) * 128, s * 64, 64)
                    q_ap = self.QT[:, h, s * 64:(s + 1) * 64]
                    for a in range(5):
                        if a < 4:
                            kt = CK[:, h, TB + a * 128: TB + (a + 1) * 128]
                            v = CV[:, 4 + a, h * 128:(h + 1) * 128]
                            nk, r0, Bk, Bv = 128, TB - 128 * a, B_CK, B_CV
                        else:
                            kt = AK[:, h, s * 64:(s + 1) * 64]
                            v = AV[0:64, s, h * 128:(h + 1) * 128]
                            nk, r0, Bk, Bv = 64, 0, B_AKc, B_AVc
                        after = (lambda G=G, h=h, s=s: self.gate_post(G, h, s * 64, 64)) if a == 4 else None
                        self.attn_tile(pipe, kt, nk, q_ap, 64, scale, v, 0, 128, s * 64, eb[0:nk, r0:r0 + 64],
                                       [Bk], [Bv], [B_eb], a == 0, a == 4, after=after)
                pipe.flush()
        self.mark("  A out")
        self.out_proj(["A%d_o0" % j, "A%d_o512" % j])

    def emit_k_out_a(self, j, h, k32, B_k32):
        S = self.S
        ko_, B_ko = self.stage.next()
        ko = ko_[:, 0:512].rearrange("p (a d) -> p a d", a=4)
        pz, B_pz = self.pz.next()
        if self.kind == "p":
            for tb in range(4):
                self.tr(pz[:, tb * 128:(tb + 1) * 128], k32[:, tb * 128:(tb + 1) * 128], self.ident[:, :],
                        [B_k32, self.B_ident], [B_pz], inc=(tb == 3))
            self.copy("act", ko[:, :, :], pz[:, :].rearrange("p (a d) -> p a d", a=4), [B_pz], [B_ko])
            S.dma("sp", self.ak_p[j, :, h * 128:(h + 1) * 128].rearrange("(a p) d -> p a d", p=128), ko[:, :, :],
                  reads=[B_ko], final=True)
        else:
            for s in range(NS):
                self.tr(pz[0:64, s * 128:(s + 1) * 128], k32[:, s * 64:(s + 1) * 64], self.ident[:, :],
                        [B_k32, self.B_ident], [B_pz], inc=(s == NS - 1))
            self.copy("act", ko[0:64, :, :], pz[0:64, :].rearrange("p (a d) -> p a d", a=4), [B_pz], [B_ko])
            S.dma("sp", self.ak_s[j, :, 448:512, h * 128:(h + 1) * 128].rearrange("s p d -> p s d"), ko[0:64, :, :],
                  reads=[B_ko], final=True)

    def load_cache_a(self, j, s, CK, CV, B_CK, B_CV):
        S = self.S
        if "nod2d" not in DBG:
            S.dma("sp", self.ak_s[j, s, 0:448, :], self.cak[j, s, 64:512, :], semkey="d2d", final=True)
            S.dma("sp", self.av_s[j, s, 0:448, :], self.cav[j, s, 64:512, :], semkey="d2d", final=True)
        S.dma("pool", CV[:, 4:8, :], self.cav[j, s, :, :].rearrange("(a p) d -> p a d", p=128), writes=[B_CV])
        for hf in range(2):
            ct_, B_ct = self.sc4.next()
            ct = ct_[:, :].rearrange("p (a d) -> p a d", a=2)
            S.dma("pool", ct[:, :, :],
                  self.cak[j, s, hf * 256:(hf + 1) * 256, :].rearrange("(a p) d -> p a d", p=128), writes=[B_ct])
            for hp in range(4):
                pt, B_pt = self.pzb()
                for hh in range(2):
                    h = hp * 2 + hh
                    for a in range(2):
                        self.tr(pt[:, hh * 256 + a * 128: hh * 256 + (a + 1) * 128], ct[:, a, h * 128:(h + 1) * 128],
                                self.identb[:, :], [B_ct, self.B_identb], [B_pt], inc=(hh == 1 and a == 1))
                self.copy("act" if hp % 2 == 0 else "dve",
                          CK[:, hp * 2:hp * 2 + 2, TB + hf * 256: TB + (hf + 1) * 256],
                          pt[:, 0:512].rearrange("p (h t) -> p h t", h=2), [B_pt], [B_CK])

    def layer_b(self):
        S = self.S
        T, kind, bi = self.T, self.kind, self.bi
        cur = bi % 2 if kind == "p" else 0
        prev = 1 - cur
        want_out = (kind == "s") or (bi == NPB - 1)
        scale = 64.0 ** -0.5
        self.rms_xn(16)
        BK, BV = self.BK, self.BV
        B_BKc, B_BVc, B_BKp, B_BVp = self.B_BK[cur], self.B_BV[cur], self.B_BK[prev], self.B_BV[prev]
        xn_fn = lambda k: self.xn[:, k, 0:T]
        qkpipe = Pipe(1)
        for sl in range(2):
            wv, B_w = self.slab("B_in%d" % (sl * 512))
            for cc in range(4):
                c = sl * 4 + cc
                pz, B_pz = self.proj_fm(wv, B_w, cc * 128, 128, T, 8, xn_fn, [self.B_xn])

                def stage2(pz=pz, B_pz=B_pz, c=c):
                    r, B_r = self.sumsq_rstd(pz, B_pz, 0, 128, T, 1, 1.0 / 64)
                    self.stt("dve", self.QT[:, c, 0:T], pz[:, 0:T], self.gp[:, 36:37], r[:, 0:T], ALU.mult, ALU.mult,
                             [B_pz, self.B_gp, B_r], [self.B_QT])

                qkpipe.push(stage2)
        wv, B_w = self.slab("B_in1024")
        if want_out:
            qkpipe.flush()
            kst_, B_kst = self.stage.next()
        for g in range(4):
            pzp = self.pz.next()
            for half in range(2):
                self.proj_fm(wv, B_w, g * 64, 64, T, 8, xn_fn, [self.B_xn], orow=half * 64, pz_pair=pzp)
            pz, B_pz = pzp
            if not want_out:
                def stage2(pz=pz, B_pz=B_pz, g=g):
                    r, B_r = self.sumsq_rstd(pz, B_pz, 0, 128, T, 1, 1.0 / 64)
                    self.stt("dve", BK[:, g, cur * TB:cur * TB + T], pz[:, 0:T], self.gp[:, 37:38], r[:, 0:T],
                             ALU.mult, ALU.mult, [B_pz, self.B_gp, B_r], [B_BKc])

                qkpipe.push(stage2)
            else:
                r, B_r = self.sumsq_rstd(pz, B_pz, 0, 128, T, 1, 1.0 / 64)
                k32, B_k32 = self.kst32.next()
                self.stt("dve", k32[:, 0:T], pz[:, 0:T], self.gp[:, 37:38], r[:, 0:T], ALU.mult, ALU.mult,
                         [B_pz, self.B_gp, B_r], [B_k32])
                self.copy("act", BK[:, g, cur * TB:cur * TB + T], k32[:, 0:T], [B_k32], [B_BKc])
                pt, B_pt = self.pz.next()
                if kind == "p":
                    self.tr(pt[:, 0:64], k32[0:64, 384:512], self.ident[0:64, 0:64], [B_k32, self.B_ident], [B_pt])
                    self.copy("act", kst_[:, g * 64:(g + 1) * 64], pt[:, 0:64], [B_pt], [B_kst])
                else:
                    for s in range(NS):
                        self.tr(pt[0:64, s * 64:(s + 1) * 64], k32[0:64, s * 64:(s + 1) * 64], self.ident[0:64, 0:64],
                                [B_k32, self.B_ident], [B_pt], inc=(s == NS - 1))
                    self.copy("act", kst_[0:64, 0:1024].rearrange("p (s c) -> p s c", s=4)[:, :, g * 64:(g + 1) * 64],
                              pt[0:64, 0:256].rearrange("p (s d) -> p s d", s=4), [B_pt], [B_kst])
        if want_out:
            if kind == "p":
                S.dma("sp", self.bk_p[:, :], kst_[:, 0:256], reads=[B_kst], final=True)
            else:
                S.dma("sp", self.bk_s[:, 64:128, :].rearrange("s p c -> p s c"),
                      kst_[0:64, 0:1024].rearrange("p (s c) -> p s c", s=4), reads=[B_kst], final=True)
        ntile = 4 if kind == "p" else NS
        rows = 128 if kind == "p" else 64
        for tb in range(ntile):
            pz, B_pz = self.pz.next()
            for k in range(8):
                self.mm(pz[0:rows, 0:256], self.xn[:, k, tb * rows:(tb + 1) * rows], wv[:, k, 256:512], k == 0, k == 7,
                        [self.B_xn, B_w], [B_pz], inc=(k == 7))
            qkpipe.flush()
            self.copy("act", BV[0:rows, cur * 4 + tb, :], pz[0:rows, 0:256], [B_pz], [B_BVc])
            if want_out and (kind == "s" or tb == 3):
                os_, B_os = self.stage.next()
                self.copy("dve", os_[0:rows, 0:256], pz[0:rows, 0:256], [B_pz], [B_os])
                dst = self.bv_p[:, :] if kind == "p" else self.bv_s[tb, 64:128, :]
                S.dma("sp", dst, os_[0:rows, 0:256], reads=[B_os], final=True)
        if kind == "s":
            S.dma("sp", self.bk_s[:, 0:64, :], self.cbk[:, 64:128, :], semkey="d2d", final=True)
            S.dma("sp", self.bv_s[:, 0:64, :], self.cbv[:, 64:128, :], semkey="d2d", final=True)
            S.dma("pool", BV[:, prev * 4:prev * 4 + 4, :], self.cbv[:, :, :].rearrange("s p d -> p s d"),
                  writes=[B_BVp])
            ct, B_ct = self.cst.next()
            S.dma("pool", ct[:, :, :], self.cbk[:, :, :].rearrange("s p d -> p s d"), writes=[B_ct])
            for g in range(4):
                pt, B_pt = self.pzb()
                for s in range(NS):
                    for half in range(2):
                        self.tr(pt[half * 64:(half + 1) * 64, s * 128:(s + 1) * 128], ct[:, s, g * 64:(g + 1) * 64],
                                self.identb[:, :], [B_ct, self.B_identb], [B_pt], inc=(s == NS - 1 and half == 1))
                self.copy("act", BK[:, g, prev * TB:(prev + 1) * TB], pt[:, 0:512], [B_pt], [B_BKp])
        self.mark("  B attn")
        pipe = Pipe(3)

        def load_eb(h):
            eb, B_eb = self.eba.next()
            S.dma("sp", eb[:, 0:256], self.EBB_d[h, :, :], reads=[self.B_EBB], writes=[B_eb])
            return eb, B_eb

        nxt_eb = load_eb(0)
        for c in range(8):
            if c % 4 == 0:
                gw, B_gw = self.slab("B_g%d" % (1536 + (c // 4) * 512))
            G = self.gate_pre(gw, B_gw, (c % 4) * 128, 0, T)
            for half in range(2):
                h = 2 * c + half
                g = h // 4
                orow = half * 64
                eb, B_eb = nxt_eb
                if h < 15:
                    nxt_eb = load_eb(h + 1)
                tl = []
                if kind == "p":
                    tiles = []
                    if bi > 0:
                        tiles.append((prev, 3, 0, 128, 128))
                    for a in range(4):
                        tiles.append((cur, a, 128 * a, min(256, T - 128 * a), 0))
                    for ti, (slot, a, q_lo, N, r0) in enumerate(tiles):
                        Bk, Bv = (B_BKc, B_BVc) if slot == cur else (B_BKp, B_BVp)
                        tl.append((BK[orow:orow + 64, g, slot * TB + a * 128: slot * TB + (a + 1) * 128], 128,
                                   self.QT[orow:orow + 64, c, q_lo:q_lo + N], N,
                                   BV[:, slot * 4 + a, g * 64:(g + 1) * 64], q_lo, eb[:, r0:r0 + N], Bk, Bv,
                                   ti == len(tiles) - 1))
                else:
                    for s in range(NS):
                        q_ap = self.QT[orow:orow + 64, c, s * 64:(s + 1) * 64]
                        tl.append((BK[orow:orow + 64, g, prev * TB + s * 128: prev * TB + (s + 1) * 128], 128, q_ap, 64,
                                   BV[:, prev * 4 + s, g * 64:(g + 1) * 64], s * 64, eb[:, 128:192], B_BKp, B_BVp,
                                   False))
                        tl.append((BK[orow:orow + 64, g, cur * TB + s * 64: cur * TB + (s + 1) * 64], 64, q_ap, 64,
                                   BV[0:64, cur * 4 + s, g * 64:(g + 1) * 64], s * 64, eb[0:64, 0:64], B_BKc, B_BVc,
                                   s == NS - 1))
                for ti, (kt, nk, q_ap, N, v, q_lo, eb_ap, Bk, Bv, lastt) in enumerate(tl):
                    zero = T if (half == 0 and ti == 0) else None
                    after = None
                    if half == 1 and ti == len(tl) - 1:
                        after = lambda G=G, c=c: self.gate_post(G, c, 0, T, esink_ap=self.esink[:, c:c + 1])
                    self.attn_tile(pipe, kt, nk, q_ap, N, scale, v, orow, 64, q_lo, eb_ap, [Bk], [Bv], [B_eb],
                                   False, lastt, zero=zero, after=after)
        pipe.flush()
        self.mark("  B out")
        self.out_proj(["B_o0", "B_o512"])

    def c_expand(self, kvb, B_kvb, ckv_fn, ckv_bufs, kr_ap, kr_bufs, n, kdst_fn, vdst, B_kdst, B_vdst):
        S = self.S
        kpipe = Pipe(1)
        for h in range(16):
            pz, B_pz = self.proj_fm(kvb, B_kvb, h * 128, 64, n, 2, ckv_fn, ckv_bufs)

            def stage2(pz=pz, B_pz=B_pz, h=h):
                r, B_r = self.sumsq_rstd(pz, B_pz, 0, 64, n, 1, 1.0 / 64)
                kt, B_kt = self.kts.next()
                self.stt("dve", kt[0:64, 0:n], pz[0:64, 0:n], self.gp[0:64, 46:47], r[0:64, 0:n], ALU.mult, ALU.mult,
                         [B_pz, self.B_gp, B_r], [B_kt])
                self.copy("pool", kt[64:96, 0:n], kr_ap, kr_bufs, [B_kt])
                S.dma("sp", kdst_fn(h), kt[0:96, 0:n], reads=[B_kt], writes=[B_kdst])

            kpipe.push(stage2)
        kpipe.flush()
        vs_, B_vs = self.sc4.next()
        ntb = (n + 127) // 128
        kv3 = kvb[:, :, :].rearrange("p k (h e) -> p k h e", e=128)
        for hf in range(2):
            vs = vs_[:, :].rearrange("p (h a d) -> p h a d", h=8, a=4)
            for tb in range(ntb):
                rows = min(128, n - tb * 128)
                pz, B_pz = self.pz.next()
                for k in range(2):
                    self.mm(pz[0:rows, :].rearrange("p (h d) -> p h d", h=8), ckv_fn(k)[:, tb * 128: tb * 128 + rows],
                            kv3[:, k, hf * 8:(hf + 1) * 8, 64:128], k == 0, k == 1, ckv_bufs + [B_kvb], [B_pz],
                            inc=(k == 1))
                self.copy("act", vs[0:rows, :, tb, :], pz[0:rows, :].rearrange("p (h d) -> p h d", h=8), [B_pz], [B_vs])
            rows = min(128, n)
            S.dma("sp", vdst(hf, rows, ntb), vs[0:rows, :, 0:ntb, :], reads=[B_vs], writes=[B_vdst])

    def layer_c(self):
        S = self.S
        T, kind, bi = self.T, self.kind, self.bi
        scale = 96.0 ** -0.5
        self.rms_xn(24)
        xn_fn = lambda k: self.xn[:, k, 0:T]
        big = self.big
        csrc = self.csP[:, :, bi * TB: bi * TB + T] if kind == "p" else self.csS[:, :, 0:T]
        S.dma("sp", self.cs[64:96, :, 0:T], csrc.rearrange("a r t -> r a t"), writes=[self.B_cs])
        wv, B_w = self.slab("C_qa")
        pss, B_pss = self.pss.next()
        for oc in range(4):
            pz, B_pz = self.proj_fm(wv, B_w, oc * 128, 128, T, 8, xn_fn, [self.B_xn])
            sq, B_sq = self.sq.next()
            self.act(sq[:, 0:T], pz[:, 0:T], AF.Square, [B_pz], [B_sq])
            self.copy("dve", big[:, oc, 0:T], pz[:, 0:T], [B_pz], [self.B_QT])
            self.mm(pss[:, 0:T], self.bd[:, 0, :], sq[:, 0:T], oc == 0, oc == 3, [self.B_bd, B_sq], [B_pss],
                    inc=(oc == 3))
        r, B_r = self.rstd_from(pss, B_pss, 0, 128, T, 1.0 / 512)
        for oc in range(4):
            self.stt("dve", self.qan[:, oc, 0:T], big[:, oc, 0:T], self.gp[:, 38 + oc:39 + oc], r[:, 0:T],
                     ALU.mult, ALU.mult, [self.B_QT, self.B_gp, B_r], [self.B_qan])
        wv, B_w = self.slab("C_kvr")
        self.copy("pool", self.wsk[:, :, 0:16], wv[:, :, 272:288], [B_w], [self.B_wsk])
        self.copy("pool", self.wsk[:, :, 16:32], wv[:, :, 256:272], [B_w], [self.B_wsk])
        pss, B_pss = self.pss.next()
        for oc in range(2):
            pz, B_pz = self.proj_fm(wv, B_w, oc * 128, 128, T, 8, xn_fn, [self.B_xn])
            sq, B_sq = self.sq.next()
            self.act(sq[:, 0:T], pz[:, 0:T], AF.Square, [B_pz], [B_sq])
            self.copy("dve", big[:, oc, 0:T], pz[:, 0:T], [B_pz], [self.B_QT])
            self.mm(pss[:, 0:T], self.bd[:, 0, :], sq[:, 0:T], oc == 0, oc == 1, [self.B_bd, B_sq], [B_pss],
                    inc=(oc == 1))
        r, B_r = self.rstd_from(pss, B_pss, 0, 128, T, 1.0 / 256)
        for oc in range(2):
            self.stt("dve", big[:, 2 + oc, 0:T], big[:, oc, 0:T], self.gp[:, 42 + oc:43 + oc], r[:, 0:T],
                     ALU.mult, ALU.mult, [self.B_QT, self.B_gp, B_r], [self.B_QT])
            self.copy("act", self.ckvT[:, oc, 0:T], big[:, 2 + oc, 0:T], [self.B_QT], [self.B_ckvT])
        dst_kv = self.ckv_p if kind == "p" else self.ckv_s
        dst_kr = self.ckr_p if kind == "p" else self.ckr_s
        row0 = bi * TB if kind == "p" else 0
        for tb in range(T // 128):
            pz, B_pz = self.pz.next()
            for oc in range(2):
                self.tr(pz[:, oc * 128:(oc + 1) * 128], big[:, 2 + oc, tb * 128:(tb + 1) * 128], self.ident[:, :],
                        [self.B_QT, self.B_ident], [B_pz], inc=(oc == 1))
            os_, B_os = self.stage.next()
            self.copy("act", os_[:, 0:256], pz[:, 0:256], [B_pz], [B_os])
            S.dma("sp", dst_kv[row0 + tb * 128: row0 + (tb + 1) * 128, :], os_[:, 0:256], reads=[B_os], final=True)
        pzA, B_pzA = self.proj_fm(wv, B_w, 256, 32, T, 8, xn_fn, [self.B_xn], orow=64)
        pzB, B_pzB = self.proj_fm(self.wsk, self.B_wsk, 0, 32, T, 8, xn_fn, [self.B_xn], orow=64)
        r, B_r = self.sumsq_rstd(pzA, B_pzA, 64, 32, T, 2, 1.0 / 32)
        kr32, B_kr32 = self.f32b.next()
        self.rope_rows(kr32, B_kr32, pzA, B_pzA, pzB, B_pzB, r, B_r, 47, 56, T)
        self.copy("act", self.krb[64:96, 0:T], kr32[64:96, 0:T], [B_kr32], [self.B_krb])
        for tb in range(T // 128):
            pz, B_pz = self.pz.next()
            self.tr(pz[:, 0:32], kr32[64:96, tb * 128:(tb + 1) * 128], self.ident[64:96, 64:96],
                    [B_kr32, self.B_ident], [B_pz])
            os_, B_os = self.stage.next()
            self.copy("act", os_[:, 0:32], pz[:, 0:32], [B_pz], [B_os])
            S.dma("sp", dst_kr[row0 + tb * 128: row0 + (tb + 1) * 128, :], os_[:, 0:32], reads=[B_os], final=True)
        self.mark("  C expand")
        kvb, B_kvb = self.slab("C_kvb")
        if kind == "p":
            self.c_expand(kvb, B_kvb, lambda k: self.ckvT[:, k, 0:T], [self.B_ckvT], self.krb[64:96, 0:T],
                          [self.B_krb], T, lambda h: self.KCp[h, :, bi * TB:(bi + 1) * TB],
                          lambda hf, rows, ntb: self.VCp[hf * 8:(hf + 1) * 8, bi, :, :].rearrange(
                              "h p (a d) -> p h a d", d=64),
                          self.B_KCp[bi], self.B_VCp[bi])
        else:
            for s in range(NS):
                for g in range(4):
                    ct, B_ct = self.cst.next()
                    S.dma("pool", ct[:, :, :],
                          self.cckv[s, g * 512:(g + 1) * 512, :].rearrange("(a p) d -> p a d", p=128), writes=[B_ct])
                    ct2, B_ct2 = self.cst2.next()
                    S.dma("pool", ct2[:, :, :],
                          self.cckr[s, g * 512:(g + 1) * 512, :].rearrange("(a p) d -> p a d", p=128), writes=[B_ct2])
                    for kc in range(2):
                        pt, B_pt = self.pzb()
                        for a in range(4):
                            self.tr(pt[:, a * 128:(a + 1) * 128], ct[:, a, kc * 128:(kc + 1) * 128], self.identb[:, :],
                                    [B_ct, self.B_identb], [B_pt], inc=(a == 3))
                        self.copy("act" if kc == 0 else "dve", self.ckc[:, kc, :], pt[:, 0:512], [B_pt], [self.B_ckc])
                    pt, B_pt = self.pzb()
                    for a in range(4):
                        self.tr(pt[64:96, a * 128:(a + 1) * 128], ct2[:, a, :], self.identb[:, :],
                                [B_ct2, self.B_identb], [B_pt], inc=(a == 3))
                    self.copy("act", self.krc[64:96, :], pt[64:96, 0:512], [B_pt], [self.B_krc])
                    self.c_expand(kvb, B_kvb, lambda k: self.ckc[:, k, :], [self.B_ckc], self.krc[64:96, :],
                                  [self.B_krc], 512, lambda h, s=s, g=g: self.KCs[s, h, :, g * 512:(g + 1) * 512],
                                  lambda hf, rows, ntb, s=s, g=g: self.VCs[s, hf * 8:(hf + 1) * 8, g, :, :].rearrange(
                                      "h p (a d) -> p h a d", d=64),
                                  self.B_KCs[s][g], self.B_VCs[s][g])
                self.c_expand(kvb, B_kvb, lambda k, s=s: self.ckvT[:, k, s * 64:(s + 1) * 64], [self.B_ckvT],
                              self.krb[64:96, s * 64:(s + 1) * 64], [self.B_krb], 64,
                              lambda h, s=s: self.KCs[s, h, :, 2048:2112],
                              lambda hf, rows, ntb, s=s: self.VCs[s, hf * 8:(hf + 1) * 8, 4, 0:64, 0:64].rearrange(
                                  "h p (a d) -> p h a d", d=64),
                              self.B_KCs[s][4], self.B_VCs[s][4])
        self.mark("  C qproj")
        hbase = 0
        qpipe = Pipe(1)
        pz_saved = self.pz
        self.pz = Rot(pz_saved.items + self.pst.items)
        for name, nh in (("C_qb0", 10), ("C_qb1", 6)):
            wv, B_w = self.slab(name)
            w4 = wv[:, :, 0:nh * 96].rearrange("p k (h e) -> p k h e", e=96)
            self.copy("pool", self.wsw[:, :, 0:nh, 0:16], w4[:, :, :, 80:96], [B_w], [self.B_wsw])
            self.copy("pool", self.wsw[:, :, 0:nh, 16:32], w4[:, :, :, 64:80], [B_w], [self.B_wsw])
            qan_fn = lambda k: self.qan[:, k, 0:T]
            for hh in range(nh):
                h = hbase + hh
                pzA, B_pzA = self.proj_fm(wv, B_w, hh * 96, 96, T, 4, qan_fn, [self.B_qan])
                pzB, B_pzB = self.pz.next()
                for k in range(4):
                    self.mm(pzB[64:96, 0:T], self.wsw[:, k, hh, :], self.qan[:, k, 0:T], k == 0, k == 3,
                            [self.B_wsw, self.B_qan], [B_pzB], inc=(k == 3))

                def stage2(pzA=pzA, B_pzA=B_pzA, pzB=pzB, B_pzB=B_pzB, h=h):
                    r, B_r = self.sumsq_rstd(pzA, B_pzA, 0, 96, T, 2, self.cpk[0:96, 0:1])
                    self.stt("dve", self.QT[0:64, h, 0:T], pzA[0:64, 0:T], self.gp[0:64, 44:45], r[0:64, 0:T],
                             ALU.mult, ALU.mult, [B_pzA, self.B_gp, B_r], [self.B_QT])
                    t32, B_t32 = self.f32b.next()
                    self.rope_rows(t32, B_t32, pzA, B_pzA, pzB, B_pzB, r, B_r, 44, 45, T)
                    self.copy("act", self.QT[64:96, h, 0:T], t32[64:96, 0:T], [B_t32], [self.B_QT])

                qpipe.push(stage2)
            qpipe.flush()
            hbase += nh
        self.pz = pz_saved
        self.mark("  C attn")
        pipe = Pipe(3)
        items = []
        for h in range(16):
            if kind == "p":
                for kg in range(bi + 1):
                    items.append((h, ("p", kg)))
            else:
                for s in range(NS):
                    for g in range(5):
                        items.append((h, ("s", s, g)))
        loaded = {}

        def issue(ii):
            h, d = items[ii]
            kc, B_kc = self.kcg.next()
            vc, B_vc = self.vcg.next()
            if d[0] == "p":
                kg = d[1]
                S.dma("sp", kc[0:96, :], self.KCp[h, :, kg * TB:(kg + 1) * TB], reads=[self.B_KCp[kg]], writes=[B_kc])
                S.dma("sp", vc[:, :, 64:128], self.VCp[h, kg, :, :].rearrange("p (a d) -> p a d", d=64),
                      reads=[self.B_VCp[kg]], writes=[B_vc])
            else:
                s_, g = d[1], d[2]
                nkeys = 512 if g < 4 else 64
                S.dma("sp", kc[0:96, 0:nkeys], self.KCs[s_, h, :, g * 512: g * 512 + nkeys],
                      reads=[self.B_KCs[s_][g]], writes=[B_kc])
                if g < 4:
                    S.dma("sp", vc[:, :, 64:128], self.VCs[s_, h, g, :, :].rearrange("p (a d) -> p a d", d=64),
                          reads=[self.B_VCs[s_][g]], writes=[B_vc])
                else:
                    S.dma("sp", vc[0:64, 0, 64:128], self.VCs[s_, h, 4, 0:64, 0:64], reads=[self.B_VCs[s_][4]],
                          writes=[B_vc])
            loaded[ii] = (kc, B_kc, vc, B_vc)

        n_issued = 0
        G = None
        for ii, (h, d) in enumerate(items):
            if ii > 0 and items[ii - 1][1][0] == "s" and items[ii - 1][1][2] == 4:
                pipe.flush()
            while n_issued < min(len(items), ii + 2):
                issue(n_issued)
                n_issued += 1
            kc, B_kc, vc, B_vc = loaded.pop(ii)
            c = h // 2
            orow = (h % 2) * 64
            first_of_head = ii == 0 or items[ii - 1][0] != h
            last_of_head = ii == len(items) - 1 or items[ii + 1][0] != h
            if first_of_head and h % 8 == 0:
                gw, B_gw = self.slab("C_g%d" % (800 + (h // 8) * 512))
            if first_of_head and h % 2 == 0:
                G = self.gate_pre(gw, B_gw, (c % 4) * 128, 0, T)
            vlo = 64 if h % 2 == 0 else 0
            acc = (self.po, self.B_po) if h % 2 == 0 else (self.pd, self.B_pd)
            if d[0] == "p":
                diag = d[1] == bi
                tl = []
                for a in range(4):
                    q_lo = 128 * a if diag else 0
                    N = T - q_lo
                    tl.append((kc[0:96, a * 128:(a + 1) * 128], 128, self.QT[0:96, h, q_lo:q_lo + N], N,
                               vc[:, a, vlo:vlo + 128], q_lo, self.maskc[:, 0:N] if diag else None, diag and a == 3))
            else:
                s_, g = d[1], d[2]
                q_ap = self.QT[0:96, h, s_ * 64:(s_ + 1) * 64]
                nk = 128 if g < 4 else 64
                tl = [(kc[0:96, a * 128: a * 128 + nk], nk, q_ap, 64, vc[0:nk, a, vlo:vlo + 128], s_ * 64, None,
                       g == 4 and s_ == NS - 1) for a in range(4 if g < 4 else 1)]
            for ti, (kt, nk, q_ap, N, v, q_lo, eb_ap, lastt) in enumerate(tl):
                after = None
                if last_of_head and h % 2 == 1 and ti == len(tl) - 1:
                    after = lambda G=G, c=c: self.gate_post_c(G, c, T)
                self.attn_tile(pipe, kt, nk, q_ap, N, scale, v, 0, 128, q_lo, eb_ap, [B_kc], [B_vc],
                               [self.B_maskc] if eb_ap is not None else [], first_of_head and ti == 0, lastt,
                               after=after, fused_acc=acc)
        pipe.flush()
        self.mark("  C out")
        self.out_proj(["C_o0", "C_o512"])

    def rope_rows(self, o32, B_o, pzA, B_pzA, pzB, B_pzB, r, B_r, gcol, gcol_sw, T):
        t2, B_t2 = self.f32a.next()
        self.stt("dve", o32[64:96, 0:T], pzA[64:96, 0:T], self.gp[64:96, gcol:gcol + 1], self.cs[64:96, 0, 0:T],
                 ALU.mult, ALU.mult, [B_pzA, self.B_gp, self.B_cs], [B_o])
        self.stt("dve", t2[64:96, 0:T], pzB[64:96, 0:T], self.gp[64:96, gcol_sw:gcol_sw + 1], self.cs[64:96, 1, 0:T],
                 ALU.mult, ALU.mult, [B_pzB, self.B_gp, self.B_cs], [B_t2])
        self.tt("dve", o32[64:96, 0:T], o32[64:96, 0:T], t2[64:96, 0:T], ALU.add, [B_o, B_t2], [B_o])
        self.tt("dve", o32[64:96, 0:T], o32[64:96, 0:T], r[64:96, 0:T], ALU.mult, [B_o, B_r], [B_o])


def _consts():
    c = {}
    c["ident"] = np.eye(128, dtype=np.float32)
    bd = np.zeros((128, 4, 128), np.float32)
    bd[:, 0, :] = 1.0
    bd[0:64, 1, 0:64] = 1.0
    bd[64:128, 1, 64:128] = 1.0
    bd[0:64, 2, 0:64] = 1.0
    bd[64:96, 2, 64:96] = 1.0
    c["bdpack"] = bd
    cp = np.zeros((128, 8), np.float32)
    cp[0:64, 0] = 1.0 / 64
    cp[64:128, 0] = 1.0 / 32
    c["cpack"] = cp
    p = np.arange(128)[:, None]
    r = np.arange(256)[None, :]
    c["distb"] = np.abs(r - p).astype(np.float32)
    mc = np.ones((128, 512), np.float32)
    mc[64:128, 0:64] = 0.0
    c["maskc"] = mc
    half = 16
    inv = (np.float32(ROPE_THETA) ** (-np.arange(half, dtype=np.float32) / np.float32(half))).astype(np.float32)

    def tables(pos):
        ang = (pos.astype(np.float32)[None, :] * inv[:, None]).astype(np.float32)
        cos = np.cos(ang.astype(np.float64)).astype(np.float32)
        sin = np.sin(ang.astype(np.float64)).astype(np.float32)
        return np.stack([np.concatenate([cos, cos], 0), np.concatenate([-sin, sin], 0)], 0)

    c["csP"] = tables(np.arange(SEQ))
    c["csS"] = np.tile(tables(PAST + np.arange(DS)), (1, 1, NS))
    return c


def _gpack(inp):
    g = np.zeros((128, GP_NCOL), np.float32)

    def chunks(v):
        return np.asarray(v, np.float32).reshape(-1, 128).T

    g[:, 0:8] = chunks(inp["a_norm"][0])
    g[:, 8:16] = chunks(inp["a_norm"][1])
    g[:, 16:24] = chunks(inp["b_norm"][0])
    g[:, 24:32] = chunks(inp["c_norm"][0])
    g[:, 32] = inp["a_q_norm"][0]
    g[:, 33] = inp["a_k_norm"][0]
    g[:, 34] = inp["a_q_norm"][1]
    g[:, 35] = inp["a_k_norm"][1]
    g[:, 36] = np.tile(inp["b_q_norm"][0], 2)
    g[:, 37] = np.tile(inp["b_k_norm"][0], 2)
    g[:, 38:42] = chunks(inp["c_q_a_norm"][0])
    g[:, 42:44] = chunks(inp["c_kv_a_norm"][0])
    qg = np.asarray(inp["c_q_norm"][0], np.float32)
    kg = np.asarray(inp["c_k_norm"][0], np.float32)
    g[0:96, 44] = qg
    g[64:80, 45] = qg[80:96]
    g[80:96, 45] = qg[64:80]
    g[0:64, 46] = kg[0:64]
    g[64:96, 47] = kg[64:96]
    g[64:80, 56] = kg[80:96]
    g[80:96, 56] = kg[64:80]
    sk = np.asarray(inp["b_sinks"][0], np.float32)
    for c_ in range(8):
        g[0:64, 48 + c_] = sk[2 * c_]
        g[64:128, 48 + c_] = sk[2 * c_ + 1]
    return g


_NC_CACHE = {}
CFG = {"layers": (0, 1, 2, 3), "npb": NPB, "do_sample": True}


def kernel(**inp):
    inp = {k: np.asarray(v) for k, v in inp.items()}
    key = (tuple(CFG["layers"]), CFG["npb"], CFG["do_sample"])
    if key not in _NC_CACHE:
        b = Builder(layers=CFG["layers"], npb=CFG["npb"], do_sample=CFG["do_sample"])
        _NC_CACHE[key] = (b.build(), b)
    nc, b = _NC_CACHE[key]
    return run_kernel(nc, b, inp)


def make_in_maps(inp):
    consts = _consts()
    gp = _gpack(inp)
    p = np.arange(128)[:, None]
    r = np.arange(640)[None, :]
    idx = np.clip(r - p, -128, 128) + 128
    relT = np.ascontiguousarray(inp["a_rel_bias"][:, :, idx]).astype(np.float32)
    shared = {
        "a_w_in": inp["a_w_in"], "a_w_out": inp["a_w_out"], "b_w_in": inp["b_w_in"][0], "b_w_out": inp["b_w_out"][0],
        "c_w_in": inp["c_w_in"][0], "c_w_qb": inp["c_w_qb"][0], "c_w_kvb": inp["c_w_kvb"][0],
        "c_w_out": inp["c_w_out"][0], "gpack": gp, "relT": relT,
    }
    shared.update(consts)
    maps = []
    for c in range(NCORES):
        sl = slice(c * NS, (c + 1) * NS)
        m = dict(shared)
        m["xp"] = inp["x_prompt"][c]
        m["xs"] = inp["x_sample"][sl].reshape(NS * DS, D)
        m["cak"] = inp["cache_a_k"][:, sl].reshape(2, NS, 512, 1024)
        m["cav"] = inp["cache_a_v"][:, sl].reshape(2, NS, 512, 1024)
        m["cbk"] = inp["cache_b_k"][0, sl].reshape(NS, 128, 256)
        m["cbv"] = inp["cache_b_v"][0, sl].reshape(NS, 128, 256)
        m["cckv"] = inp["cache_c_kv"][0, sl]
        m["cckr"] = inp["cache_c_kr"][0, sl]
        maps.append({k: np.ascontiguousarray(v, dtype=np.float32) for k, v in m.items()})
    return maps


def run_kernel(nc, b, inp, trace=False):
    maps = make_in_maps(inp)
    maps = [{k: m[k] for k in b.in_names} for m in maps]
    res = run_bass_kernel_spmd(nc, maps, core_ids=list(range(NCORES)), trace=trace)
    R = res.results

    def cat(name, shape_per_core, axis):
        return np.stack([R[c][name].reshape(shape_per_core) for c in range(NCORES)], axis=axis)

    y_p = cat("y_p", (SEQ, D), 0)
    y_s = np.concatenate([R[c]["y_s"].reshape(NS, DS, D) for c in range(NCORES)], 0)
    ak_p = cat("ak_p", (2, 512, 8, 128), 1)
    av_p = cat("av_p", (2, 512, 8, 128), 1)
    bk_p = cat("bk_p", (128, 4, 64), 0)[None]
    bv_p = cat("bv_p", (128, 4, 64), 0)[None]
    ckv_p = cat("ckv_p", (SEQ, 256), 0)[None]
    ckr_p = cat("ckr_p", (SEQ, 32), 0)[None]
    ak_s = np.concatenate([R[c]["ak_s"].reshape(2, NS, 512, 8, 128) for c in range(NCORES)], 1)
    av_s = np.concatenate([R[c]["av_s"].reshape(2, NS, 512, 8, 128) for c in range(NCORES)], 1)
    bk_s = np.concatenate([R[c]["bk_s"].reshape(NS, 128, 4, 64) for c in range(NCORES)], 0)[None]
    bv_s = np.concatenate([R[c]["bv_s"].reshape(NS, 128, 4, 64) for c in range(NCORES)], 0)[None]
    ckv_s = np.concatenate([R[c]["ckv_s"].reshape(NS, DS, 256) for c in range(NCORES)], 0)[None]
    ckr_s = np.concatenate([R[c]["ckr_s"].reshape(NS, DS, 32) for c in range(NCORES)], 0)[None]
    outs = (y_p, y_s, ak_p, av_p, bk_p, bv_p, ckv_p, ckr_p, ak_s, av_s, bk_s, bv_s, ckv_s, ckr_s)
    outs = tuple(np.ascontiguousarray(o, dtype=np.float32) for o in outs)
    if trace:
        return outs, res
    return outs
```

```python
import contextlib
import numpy as np
import concourse.bass as bass
import concourse.mybir as mybir
from concourse.bass_utils import run_bass_kernel_spmd

F32 = mybir.dt.float32
BF16 = mybir.dt.bfloat16
AF = mybir.ActivationFunctionType
ALU = mybir.AluOpType

NCORES = 8
DBG = set()
MASK_ENG = "pool"
D = 1024
SEQ = 4096
TB = 512
NPB = SEQ // TB
NS = 4
DS = 64
PAST = 2048
EPS = 1e-6
ROPE_THETA = 10000.0
LAYER_KINDS = ("A", "B", "C", "A")


class Buf:
    __slots__ = ("name", "w", "r", "wsem", "rsem", "dram", "excl")

    def __init__(self, name, dram=False, excl=False):
        self.name = name
        self.dram = dram
        self.excl = excl
        self.w = {}
        self.r = {}
        self.wsem = None
        self.rsem = None


class Sched:
    ENGS = ("pe", "act", "dve", "pool", "sp")

    def __init__(self, nc, stack):
        self.nc = nc
        self.stack = stack
        self.prog = {e: [] for e in self.ENGS}
        self.cnt = {e: 0 for e in self.ENGS}
        self.pending = {e: False for e in self.ENGS}
        self.seen = {e: {} for e in self.ENGS}
        self.know = {}
        self.sems = {}
        self.dcnt = {}
        self.final_tokens = {}
        for e in self.ENGS:
            self.sem("E_" + e)

    def sem(self, key):
        if key not in self.sems:
            self.sems[key] = self.stack.enter_context(self.nc.semaphore("s%d" % len(self.sems)))
        return self.sems[key]

    def _deps(self, eng, reads, writes):
        own = "E_" + eng
        m = {}
        for b in reads:
            for k, v in b.w.items():
                if k == own and eng == "pe":
                    continue
                if v > m.get(k, 0):
                    m[k] = v
            if b.excl:
                for k, v in b.r.items():
                    if k == own:
                        continue
                    if v > m.get(k, 0):
                        m[k] = v
        for b in writes:
            for src in (b.w, b.r):
                for k, v in src.items():
                    if k == own and eng == "pe":
                        continue
                    if v > m.get(k, 0):
                        m[k] = v
        seen = self.seen[eng]
        need = [(k, v) for k, v in m.items() if seen.get(k, 0) < v]
        kn = self.know
        out = []
        for i, (k, v) in enumerate(need):
            implied = False
            for j, (k2, v2) in enumerate(need):
                if j == i:
                    continue
                kk = kn.get((k2, v2))
                if kk is not None and kk.get(k, 0) >= v:
                    kk1 = kn.get((k, v))
                    if kk1 is not None and kk1.get(k2, 0) >= v2 and i < j:
                        continue
                    implied = True
                    break
            if not implied:
                out.append((k, v))
        for k, v in out:
            if seen.get(k, 0) < v:
                seen[k] = v
            kk = kn.get((k, v))
            if kk is not None:
                for k3, v3 in kk.items():
                    if seen.get(k3, 0) < v3:
                        seen[k3] = v3
        return out

    def op(self, eng, fn, reads=(), writes=(), inc=True):
        waits = self._deps(eng, reads, writes)
        key = "E_" + eng
        if inc:
            self.cnt[eng] += 1
            idx = self.cnt[eng]
            self.pending[eng] = False
            self.prog[eng].append((waits, fn, (key, 1)))
            vc = dict(self.seen[eng])
            vc[key] = idx
            self.know[(key, idx)] = vc
        else:
            idx = self.cnt[eng] + 1
            self.pending[eng] = True
            self.prog[eng].append((waits, fn, None))
        for b in reads:
            if b.r.get(key, 0) < idx:
                b.r[key] = idx
        for b in writes:
            if b.w.get(key, 0) < idx:
                b.w[key] = idx

    def dma(self, q, out_ap, in_ap, reads=(), writes=(), semkey=None, final=False):
        waits = self._deps(q, reads, writes)
        if semkey is None:
            sw = [b for b in writes if not b.dram]
            sr = [b for b in reads if not b.dram]
            qk = "sw" if q == "pool" else "hw"
            if sw:
                semkey = "W_%s_%s" % (sw[0].name, qk)
            else:
                semkey = "R_%s_%s" % (sr[0].name, qk)
        self.sem(semkey)
        self.dcnt[semkey] = self.dcnt.get(semkey, 0) + 16
        val = self.dcnt[semkey]

        def fn(e, out_ap=out_ap, in_ap=in_ap):
            return e.dma_start(out=out_ap, in_=in_ap)

        self.prog[q].append((waits, fn, (semkey, 16)))
        self.know[(semkey, val)] = dict(self.seen[q])
        for b in reads:
            if b.r.get(semkey, 0) < val:
                b.r[semkey] = val
        for b in writes:
            if b.w.get(semkey, 0) < val:
                b.w[semkey] = val
        if final:
            self.final_tokens[semkey] = val

    def emit(self, block):
        engmap = {"pe": block.tensor, "act": block.scalar, "dve": block.vector, "pool": block.gpsimd,
                  "sp": block.sync}
        for e in self.ENGS:
            assert not self.pending[e], e
        fw = [(k, v) for k, v in self.final_tokens.items()]
        sems = self.sems
        for e in self.ENGS:
            prog = self.prog[e]

            def body(eng, prog=prog, e=e):
                for waits, fn, upd in prog:
                    for k, v in waits[:-1]:
                        eng.wait_ge(sems[k], v)
                    ins = fn(eng)
                    if waits:
                        ins._wait_ge(sems[waits[-1][0]], v if False else waits[-1][1])
                    if upd is not None:
                        ins.then_inc(sems[upd[0]], upd[1])
                if e == "sp":
                    for k, v in fw:
                        eng.wait_ge(sems[k], v)

            engmap[e](body)


class Pipe:
    def __init__(self, depth=1):
        self.q = []
        self.depth = depth

    def push(self, fn):
        self.q.append(fn)
        while len(self.q) > self.depth:
            self.q.pop(0)()

    def flush(self):
        while self.q:
            self.q.pop(0)()


class Rot:
    def __init__(self, items):
        self.items = items
        self.i = 0

    def next(self):
        it = self.items[self.i % len(self.items)]
        self.i += 1
        return it


GP_NCOL = 58


class Builder:
    def __init__(self, layers=(0, 1, 2, 3), npb=NPB, do_sample=True):
        self.layers = tuple(layers)
        self.npb = npb
        self.do_sample = do_sample
        self.nc = bass.Bass("TRN2", target_bir_lowering=False)
        self.st = contextlib.ExitStack()
        self.S = Sched(self.nc, self.st)
        self.in_names = []
        self.out_names = []

    def din(self, name, shape, dt=F32):
        self.in_names.append(name)
        return self.nc.dram_tensor(name, list(shape), dt, kind="ExternalInput").ap()

    def dout(self, name, shape):
        self.out_names.append(name)
        return self.nc.dram_tensor(name, list(shape), F32, kind="ExternalOutput").ap()

    def dscratch(self, name, shape, dt):
        return self.nc.dram_tensor(name, list(shape), dt).ap()

    def sb(self, name, shape, dt):
        return self.st.enter_context(self.nc.sbuf_tensor(name, list(shape), dt))

    def ps(self, name, shape, dt=F32):
        return self.st.enter_context(self.nc.psum_tensor(name, list(shape), dt))

    def rot_sb(self, name, n, shape, dt):
        return Rot([(self.sb("%s%d" % (name, i), shape, dt), Buf("%s%d" % (name, i))) for i in range(n)])

    def mark(self, label):
        if not hasattr(self, "marks"):
            self.marks = []
        self.marks.append((label, len(self.S.prog["pe"])))

    def mm(self, out, lhsT, rhs, start, stop, reads, writes, inc=True):
        self.S.op("pe", lambda e: e.matmul(out, lhsT, rhs, start=start, stop=stop), reads, writes, inc=inc)

    def tr(self, out, in_, ident, reads, writes, inc=True):
        self.S.op("pe", lambda e: e.transpose(out, in_, ident), reads, writes, inc=inc)

    def act(self, out, in_, func, reads, writes, scale=1.0, bias=0.0):
        self.S.op("act", lambda e: e.activation(out=out, in_=in_, func=func, bias=bias, scale=scale), reads, writes)

    def copy(self, eng, out, in_, reads, writes):
        if eng == "act":
            self.S.op("act", lambda e: e.copy(out, in_), reads, writes)
        else:
            self.S.op(eng, lambda e: e.tensor_copy(out, in_), reads, writes)

    def tt(self, eng, out, in0, in1, op, reads, writes):
        self.S.op(eng, lambda e: e.tensor_tensor(out=out, in0=in0, in1=in1, op=op), reads, writes)

    def stt(self, eng, out, in0, scalar, in1, op0, op1, reads, writes):
        self.S.op(eng, lambda e: e.scalar_tensor_tensor(out=out, in0=in0, scalar=scalar, in1=in1, op0=op0, op1=op1),
                  reads, writes)

    def tsadd(self, eng, out, in0, s1, reads, writes):
        self.S.op(eng, lambda e: e.tensor_scalar_add(out, in0, s1), reads, writes)

    def recip(self, out, in_, reads, writes):
        self.S.op("dve", lambda e: e.reciprocal(out, in_), reads, writes)

    def memset(self, eng, ap, val, writes):
        self.S.op(eng, lambda e: e.memset(ap, val), (), writes)

    def build(self):
        nc, S = self.nc, self.S
        self.xp = self.din("xp", [SEQ, D])
        self.xs = self.din("xs", [NS * DS, D])
        self.cak = self.din("cak", [2, NS, 512, 1024])
        self.cav = self.din("cav", [2, NS, 512, 1024])
        self.cbk = self.din("cbk", [NS, 128, 256])
        self.cbv = self.din("cbv", [NS, 128, 256])
        self.cckv = self.din("cckv", [NS, PAST, 256])
        self.cckr = self.din("cckr", [NS, PAST, 32])
        self.a_w_in = self.din("a_w_in", [2, D, 4096])
        self.a_w_out = self.din("a_w_out", [2, D, D])
        self.b_w_in = self.din("b_w_in", [D, 2560])
        self.b_w_out = self.din("b_w_out", [D, D])
        self.c_w_in = self.din("c_w_in", [D, 1824])
        self.c_w_qb = self.din("c_w_qb", [512, 1536])
        self.c_w_kvb = self.din("c_w_kvb", [256, 2048])
        self.c_w_out = self.din("c_w_out", [D, D])
        self.gpack_d = self.din("gpack", [128, GP_NCOL])
        self.relT = self.din("relT", [2, 8, 128, 640])
        self.cpack_d = self.din("cpack", [128, 8])
        self.ident_d = self.din("ident", [128, 128])
        self.bd_d = self.din("bdpack", [128, 4, 128])
        self.distb_d = self.din("distb", [128, 256])
        self.maskc_d = self.din("maskc", [128, 512])
        self.csP = self.din("csP", [2, 32, SEQ])
        self.csS = self.din("csS", [2, 32, NS * DS])

        self.y_p = self.dout("y_p", [SEQ, D])
        self.y_s = self.dout("y_s", [NS * DS, D])
        self.ak_p = self.dout("ak_p", [2, 512, 1024])
        self.av_p = self.dout("av_p", [2, 512, 1024])
        self.bk_p = self.dout("bk_p", [128, 256])
        self.bv_p = self.dout("bv_p", [128, 256])
        self.ckv_p = self.dout("ckv_p", [SEQ, 256])
        self.ckr_p = self.dout("ckr_p", [SEQ, 32])
        self.ak_s = self.dout("ak_s", [2, NS, 512, 1024])
        self.av_s = self.dout("av_s", [2, NS, 512, 1024])
        self.bk_s = self.dout("bk_s", [NS, 128, 256])
        self.bv_s = self.dout("bv_s", [NS, 128, 256])
        self.ckv_s = self.dout("ckv_s", [NS * DS, 256])
        self.ckr_s = self.dout("ckr_s", [NS * DS, 32])

        self.KCp = self.dscratch("KCp", [16, 96, SEQ], BF16)
        self.VCp = self.dscratch("VCp", [16, NPB, 128, 256], BF16)
        self.KCs = self.dscratch("KCs", [NS, 16, 96, 2560], BF16)
        self.VCs = self.dscratch("VCs", [NS, 16, 5, 128, 256], BF16)
        self.EBA_d = self.dscratch("EBA", [2, 8, 128, 640], BF16)
        self.EBB_d = self.dscratch("EBBd", [16, 128, 256], BF16)
        self.B_KCp = [Buf("KCp%d" % g, True) for g in range(NPB)]
        self.B_VCp = [Buf("VCp%d" % g, True) for g in range(NPB)]
        self.B_KCs = [[Buf("KCs%d_%d" % (s, g), True) for g in range(5)] for s in range(NS)]
        self.B_VCs = [[Buf("VCs%d_%d" % (s, g), True) for g in range(5)] for s in range(NS)]
        self.B_EBA = Buf("EBAd", True)
        self.B_EBB = Buf("EBBd", True)

        self.xT = self.sb("xT", [128, 8, TB], F32); self.B_xT = Buf("xT")
        self.xn = self.sb("xn", [128, 8, TB], BF16); self.B_xn = Buf("xn")
        self.gT = self.sb("gT", [128, 8, TB], BF16); self.B_gT = Buf("gT")
        self.QTt = self.sb("QT", [128, 16 * TB], BF16); self.B_QT = Buf("QT")
        self.QT = self.QTt[:, :].rearrange("p (h t) -> p h t", h=16)
        self.big = self.QTt[:, :].bitcast(F32).rearrange("p (c t) -> p c t", c=8)
        self.AK = [self.sb("AK%d" % j, [128, 8, 2 * TB], BF16) for j in range(2)]
        self.AV = [self.sb("AV%d" % j, [128, 8, 1024], BF16) for j in range(2)]
        self.B_AK = [[Buf("AK%d_%d" % (j, s)) for s in range(2)] for j in range(2)]
        self.B_AV = [[Buf("AV%d_%d" % (j, s)) for s in range(2)] for j in range(2)]
        self.BK = self.sb("BK", [128, 4, 2 * TB], BF16); self.B_BK = [Buf("BK0"), Buf("BK1")]
        self.BV = self.sb("BV", [128, 8, 256], BF16); self.B_BV = [Buf("BV0"), Buf("BV1")]
        self.slabs = self.rot_sb("slab", 2, [128, 4096], BF16)
        self.wsw = self.sb("wsw", [128, 4, 16, 32], BF16); self.B_wsw = Buf("wsw")
        self.wsk = self.sb("wsk", [128, 8, 32], BF16); self.B_wsk = Buf("wsk")
        self.gp = self.sb("gp", [128, GP_NCOL], F32); self.B_gp = Buf("gp")
        self.esink = self.sb("esink", [128, 8], F32); self.B_esink = Buf("esink")
        self.cpk = self.sb("cpk", [128, 8], F32); self.B_cpk = Buf("cpk")
        self.ident = self.sb("ident_sb", [128, 128], F32); self.B_ident = Buf("ident")
        self.identb = self.sb("identb", [128, 128], BF16); self.B_identb = Buf("identb")
        self.bd = self.sb("bd", [128, 4, 128], BF16); self.B_bd = Buf("bd")
        self.zer = self.sb("zer", [128, TB], BF16); self.B_zer = Buf("zer")
        self.maskc = self.sb("maskc_sb", [128, TB], BF16); self.B_maskc = Buf("maskc")
        self.cs = self.sb("cs", [128, 2, TB], F32); self.B_cs = Buf("cs")
        self.eba = self.rot_sb("eba", 2, [128, 640], BF16)
        self.sq = self.rot_sb("sq", 2, [128, TB], BF16)
        self.f32a = self.rot_sb("fa", 2, [128, TB], F32)
        self.f32b = self.rot_sb("fb", 2, [128, TB], F32)
        self.Eb = self.rot_sb("E", 4, [128, TB], BF16)
        self.stage = self.rot_sb("stg", 2, [128, D], F32)
        self.po32 = self.rot_sb("po32", 2, [128, TB], F32)
        self.kst32 = Rot([self.f32b.items[0]])
        self.qan = self.sb("qan", [128, 4, TB], BF16); self.B_qan = Buf("qan")
        self.ckvT = self.sb("ckvT", [128, 2, TB], BF16); self.B_ckvT = Buf("ckvT")
        self.ckc = self.sb("ckc", [128, 2, TB], BF16); self.B_ckc = Buf("ckc")
        self.krb = self.sb("krb", [128, TB], BF16); self.B_krb = Buf("krb")
        self.krc = self.sb("krc", [128, TB], BF16); self.B_krc = Buf("krc")
        self.kts = self.rot_sb("kts", 2, [128, TB], BF16)
        self.sc4 = self.rot_sb("sc4", 1, [128, 2048], BF16)
        self.kcg = self.rot_sb("kcg", 3, [128, TB], BF16)
        self.vcg = self.rot_sb("vcg", 3, [128, 4, 192], BF16)
        self.cst = self.rot_sb("cst", 1, [128, 4, 256], BF16)
        self.cst2 = self.rot_sb("cst2", 1, [128, 4, 32], BF16)
        self.pz = Rot([(self.ps("pz%d" % i, [128, TB]), Buf("pz%d" % i, excl=True)) for i in range(3)])
        self.pss = Rot([(self.ps("pss0", [128, TB]), Buf("pss0", excl=True))])
        self.pst = Rot([(self.ps("pst%d" % i, [128, TB]), Buf("pst%d" % i, excl=True)) for i in range(2)])
        self.pst3 = Rot(self.pst.items + [self.pz.items[2]])
        self.pzg = Rot(self.pz.items[0:2])
        self.po = self.ps("po", [128, TB]); self.B_po = Buf("po", excl=True)
        self.pd = self.ps("pd", [128, TB]); self.B_pd = Buf("pd", excl=True)

        self.setup_consts()
        self.plan_weights()
        blocks = [("p", i) for i in range(self.npb)]
        if self.do_sample:
            blocks.append(("s", 0))
        for kind, i in blocks:
            self.run_block(kind, i)
        assert self.w_used == len(self.wplan), (self.w_used, len(self.wplan))
        with nc.Block() as block:
            S.emit(block)
        self.st.close()
        return nc

    def setup_consts(self):
        S = self.S
        ck = "const"
        S.dma("sp", self.gp[:], self.gpack_d[:, :], writes=[self.B_gp], semkey=ck)
        S.dma("sp", self.cpk[:], self.cpack_d[:, :], writes=[self.B_cpk], semkey=ck)
        S.dma("sp", self.ident[:], self.ident_d[:, :], writes=[self.B_ident], semkey=ck)
        S.dma("pool", self.identb[:], self.ident_d[:, :], writes=[self.B_identb], semkey="constp")
        S.dma("pool", self.bd[:], self.bd_d[:, :, :], writes=[self.B_bd], semkey="constp")
        S.dma("pool", self.maskc[:], self.maskc_d[:, :], writes=[self.B_maskc], semkey="constp")
        tot = S.dcnt[ck]
        for b in (self.B_gp, self.B_cpk, self.B_ident):
            b.w[ck] = tot
        tot = S.dcnt["constp"]
        for b in (self.B_identb, self.B_bd, self.B_maskc):
            b.w["constp"] = tot
        self.memset("dve", self.zer[:], 0.0, [self.B_zer])
        for vt, B_vt in self.vcg.items:
            self.memset("dve", vt[:, :, :], 1.0, [B_vt])
        self.act(self.esink[:], self.gp[:, 48:56], AF.Exp, [self.B_gp], [self.B_esink])
        if 1 in self.layers:
            dist, B_dist = self.f32a.next()
            S.dma("sp", dist[:, 0:256], self.distb_d[:, :], writes=[B_dist])
            for h in range(16):
                slope = float(2.0 ** (-8.0 * (h + 1) / 16))
                e, B_e = self.eba.next()
                self.act(e[:, 0:256], dist[:, 0:256], AF.Exp, [B_dist], [B_e], scale=-slope)
                self.memset("dve", e[0:64, 192:256], 0.0, [B_e])
                self.memset("dve", e[64:128, 0:64], 0.0, [B_e])
                S.dma("sp", self.EBB_d[h, :, :], e[:, 0:256], reads=[B_e], writes=[self.B_EBB])
        for j in range(2):
            if (3 * j) not in self.layers:
                continue
            for h in range(8):
                t32, B_t32 = self.stage.next()
                S.dma("sp", t32[:, 0:640], self.relT[j, h, :, :], writes=[B_t32])
                e, B_e = self.eba.next()
                self.act(e[:, :], t32[:, 0:640], AF.Exp, [B_t32], [B_e])
                self.memset("dve", e[0:64, 576:640], 0.0, [B_e])
                self.memset("dve", e[64:128, 0:64], 0.0, [B_e])
                S.dma("sp", self.EBA_d[j, h, :, :], e[:, :], reads=[B_e], writes=[self.B_EBA])

    def layer_slabs(self, li):
        kind = LAYER_KINDS[li]
        j = li // 3
        out = []
        if kind == "A":
            w = self.a_w_in[j]
            for c0 in range(0, 3072, 512):
                out.append(("A%d_in%d" % (j, c0), w[:, c0:c0 + 512], 8, 512))
            for c0 in (3072, 3584):
                out.append(("A%d_g%d" % (j, c0), w[:, c0:c0 + 512], 8, 512))
            for c0 in (0, 512):
                out.append(("A%d_o%d" % (j, c0), self.a_w_out[j][:, c0:c0 + 512], 8, 512))
        elif kind == "B":
            w = self.b_w_in
            for c0 in (0, 512, 1024):
                out.append(("B_in%d" % c0, w[:, c0:c0 + 512], 8, 512))
            for c0 in (1536, 2048):
                out.append(("B_g%d" % c0, w[:, c0:c0 + 512], 8, 512))
            for c0 in (0, 512):
                out.append(("B_o%d" % c0, self.b_w_out[:, c0:c0 + 512], 8, 512))
        else:
            w = self.c_w_in
            out.append(("C_qa", w[:, 0:512], 8, 512))
            out.append(("C_kvr", w[:, 512:800], 8, 288))
            out.append(("C_kvb", self.c_w_kvb[:, :], 2, 2048))
            out.append(("C_qb0", self.c_w_qb[:, 0:960], 4, 960))
            out.append(("C_qb1", self.c_w_qb[:, 960:1536], 4, 576))
            for c0 in (800, 1312):
                out.append(("C_g%d" % c0, w[:, c0:c0 + 512], 8, 512))
            for c0 in (0, 512):
                out.append(("C_o%d" % c0, self.c_w_out[:, c0:c0 + 512], 8, 512))
        return out

    def plan_weights(self):
        nblk = self.npb + (1 if self.do_sample else 0)
        self.wplan = []
        for _ in range(nblk):
            for li in self.layers:
                self.wplan.extend(self.layer_slabs(li))
        self.w_issued = 0
        self.w_used = 0
        self.w_live = {}
        self.wscratch = {}

    def _issue_slab(self):
        name, src, nk, ncol = self.wplan[self.w_issued]
        t, B = self.slabs.next()
        view = t[:, 0:nk * ncol].rearrange("p (k c) -> p k c", k=nk)
        if name not in self.wscratch:
            self.S.dma("pool", view, src.rearrange("(k p) c -> p k c", p=128), writes=[B])
            ws = self.dscratch("ws_" + name, [128, nk * ncol], BF16)
            Bws = Buf("ws_" + name, True)
            self.wscratch[name] = (ws, Bws)
            self.S.dma("sp", ws[:, :], t[:, 0:nk * ncol], reads=[B], writes=[Bws])
        else:
            ws, Bws = self.wscratch[name]
            self.S.dma("pool", t[:, 0:nk * ncol], ws[:, :], reads=[Bws], writes=[B])
        self.w_live[self.w_issued] = (view, B)
        self.w_issued += 1

    def slab(self, name, ahead=1):
        idx = self.w_used
        assert self.wplan[idx][0] == name, (self.wplan[idx][0], name)
        while self.w_issued < min(len(self.wplan), idx + 1 + ahead):
            self._issue_slab()
        self.w_used += 1
        return self.w_live.pop(idx)

    def run_block(self, kind, i):
        S = self.S
        T = TB if kind == "p" else NS * DS
        self.kind, self.bi, self.T = kind, i, T
        src = self.xp if kind == "p" else self.xs
        row0 = i * TB if kind == "p" else 0
        for tb in range(T // 128):
            xs_, B_xs = self.stage.next()
            S.dma("sp", xs_[:, :], src[row0 + tb * 128: row0 + (tb + 1) * 128, :], writes=[B_xs])
            for half in range(2):
                pz, B_pz = self.pz.next()
                for c in range(4):
                    cc = half * 4 + c
                    self.tr(pz[:, c * 128:(c + 1) * 128], xs_[:, cc * 128:(cc + 1) * 128], self.ident[:, :],
                            [B_xs, self.B_ident], [B_pz], inc=(c == 3))
                self.copy("act" if half == 0 else "dve",
                          self.xT[:, half * 4:half * 4 + 4, tb * 128:(tb + 1) * 128],
                          pz[:, :].rearrange("p (c t) -> p c t", c=4), [B_pz], [self.B_xT])
        for li in self.layers:
            k = LAYER_KINDS[li]
            self.mark("%s%d L%d %s rms+proj" % (kind, i, li, k))
            if k == "A":
                self.layer_a(li // 3)
            elif k == "B":
                self.layer_b()
            else:
                self.layer_c()
        self.mark("%s%d store" % (kind, i))
        dst = self.y_p if kind == "p" else self.y_s
        for tb in range(T // 128):
            os_, B_os = self.stage.next()
            for half in range(2):
                pz, B_pz = self.pz.next()
                for c in range(4):
                    cc = half * 4 + c
                    self.tr(pz[:, c * 128:(c + 1) * 128], self.xT[:, cc, tb * 128:(tb + 1) * 128], self.ident[:, :],
                            [self.B_xT, self.B_ident], [B_pz], inc=(c == 3))
                self.copy("act" if half == 0 else "dve", os_[:, half * 512:(half + 1) * 512], pz[:, :],
                          [B_pz], [B_os])
            S.dma("sp", dst[row0 + tb * 128: row0 + (tb + 1) * 128, :], os_[:, :], reads=[B_os], final=True)

    def rstd_from(self, pss, B_pss, r0, rows, T, scale):
        r, B_r = self.f32a.next()
        rd = [B_pss] if not hasattr(scale, "shape") else [B_pss, self.B_cpk]
        self.act(r[r0:r0 + rows, 0:T], pss[r0:r0 + rows, 0:T], AF.Ln, rd, [B_r], scale=scale, bias=EPS)
        self.act(r[r0:r0 + rows, 0:T], r[r0:r0 + rows, 0:T], AF.Exp, [B_r], [B_r], scale=-0.5)
        return r, B_r

    def rms_xn(self, gcol0):
        T = self.T
        sqx, B_sqx = self.gT, self.B_gT
        self.act(sqx[:, :, 0:T], self.xT[:, :, 0:T], AF.Square, [self.B_xT], [B_sqx])
        pss, B_pss = self.pss.next()
        for c in range(8):
            self.mm(pss[:, 0:T], self.bd[:, 0, :], sqx[:, c, 0:T], c == 0, c == 7, [self.B_bd, B_sqx], [B_pss],
                    inc=(c == 7))
        r, B_r = self.rstd_from(pss, B_pss, 0, 128, T, 1.0 / D)
        for c in range(8):
            self.stt("dve", self.xn[:, c, 0:T], self.xT[:, c, 0:T], self.gp[:, gcol0 + c:gcol0 + c + 1], r[:, 0:T],
                     ALU.mult, ALU.mult, [self.B_xT, self.B_gp, B_r], [self.B_xn])

    def proj_fm(self, wv, B_w, col0, M, n, nk, rhs_fn, rhs_bufs, orow=0, pz_pair=None):
        pz, B_pz = self.pz.next() if pz_pair is None else pz_pair
        for k in range(nk):
            self.mm(pz[orow:orow + M, 0:n], wv[:, k, col0:col0 + M], rhs_fn(k), k == 0, k == nk - 1,
                    [B_w] + rhs_bufs, [B_pz], inc=(k == nk - 1))
        return pz, B_pz

    def sumsq_rstd(self, pz, B_pz, r0, rows, n, bd_idx, scale):
        sq, B_sq = self.sq.next()
        self.act(sq[r0:r0 + rows, 0:n], pz[r0:r0 + rows, 0:n], AF.Square, [B_pz], [B_sq])
        pss, B_pss = self.pss.next()
        self.mm(pss[r0:r0 + rows, 0:n], self.bd[r0:r0 + rows, bd_idx, r0:r0 + rows], sq[r0:r0 + rows, 0:n], True, True,
                [self.B_bd, B_sq], [B_pss])
        return self.rstd_from(pss, B_pss, r0, rows, n, scale)

    def attn_tile(self, pipe, kt, nk, q, N, scale, v, orow, M, q_lo, eb, rk, rv, reb, first, last,
                  zero=None, after=None, fused_acc=None):
        pst, B_pst = self.pst3.next()
        self.mm(pst[0:nk, 0:N], kt, q, True, True, rk + [self.B_QT], [B_pst])
        P, B_P = self.Eb.next()
        self.act(P[0:nk, 0:N], pst[0:nk, 0:N], AF.Exp, [B_pst], [B_P], scale=scale)
        if eb is not None:
            self.tt(MASK_ENG, P[0:nk, 0:N], P[0:nk, 0:N], eb, ALU.mult, [B_P] + reb, [B_P])

        def stage_b():
            if fused_acc is not None:
                acc, B_acc = fused_acc
                self.mm(acc[:, q_lo:q_lo + N], v, P[0:nk, 0:N], first, last, rv + [B_P], [B_acc])
            else:
                if zero is not None:
                    self.zero_acc(zero)
                self.mm(self.po[orow:orow + M, q_lo:q_lo + N], v, P[0:nk, 0:N], first, last, rv + [B_P],
                        [self.B_po], inc=False)
                self.mm(self.pd[orow:orow + M, q_lo:q_lo + N], self.bd[0:nk, 0, 0:M], P[0:nk, 0:N], first, last,
                        [self.B_bd, B_P], [self.B_pd])
            if after is not None:
                after()

        pipe.push(stage_b)

    def gate_post_c(self, G, chunk, n):
        pz, B_pz, t, B_t = G
        o32, B_o32 = self.po32.next()
        self.copy("dve", o32[0:64, 0:n], self.po[0:64, 0:n], [self.B_po], [B_o32])
        self.copy("dve", o32[64:128, 0:n], self.pd[64:128, 0:n], [self.B_pd], [B_o32])
        self.stt("dve", t[0:64, 0:n], t[0:64, 0:n], 1.0, self.po[64:128, 0:n], ALU.add, ALU.mult,
                 [B_t, self.B_po], [B_t])
        self.stt("dve", t[64:128, 0:n], t[64:128, 0:n], 1.0, self.pd[0:64, 0:n], ALU.add, ALU.mult,
                 [B_t, self.B_pd], [B_t])
        self.act(t[:, 0:n], t[:, 0:n], AF.Ln, [B_t], [B_t])
        self.act(t[:, 0:n], t[:, 0:n], AF.Exp, [B_t], [B_t], scale=-1.0)
        self.tt("dve", t[:, 0:n], t[:, 0:n], pz[:, 0:n], ALU.mult, [B_t, B_pz], [B_t])
        self.tt("dve", self.gT[:, chunk, 0:n], t[:, 0:n], o32[:, 0:n], ALU.mult, [B_t, B_o32], [self.B_gT])

    def zero_acc(self, T):
        self.mm(self.po[:, 0:T], self.zer[:, 0:128], self.zer[:, 0:T], True, False, [self.B_zer], [self.B_po],
                inc=False)
        self.mm(self.pd[:, 0:T], self.zer[:, 0:128], self.zer[:, 0:T], True, False, [self.B_zer], [self.B_pd])

    def gate_pre(self, gw, B_gw, gcol, c0, n):
        pz, B_pz = self.proj_fm(gw, B_gw, gcol, 128, n, 8, lambda k: self.xn[:, k, c0:c0 + n], [self.B_xn],
                                pz_pair=self.pzg.next())
        t, B_t = self.f32b.next()
        self.act(t[:, 0:n], pz[:, 0:n], AF.Exp, [B_pz], [B_t], scale=-1.0)
        return pz, B_pz, t, B_t

    def gate_post(self, G, chunk, c0, n, esink_ap=None):
        pz, B_pz, t, B_t = G
        o32, B_o32 = self.po32.next()
        self.copy("act" if esink_ap is None else "dve", o32[:, 0:n], self.po[:, c0:c0 + n], [self.B_po], [B_o32])
        if esink_ap is not None:
            d, B_d = self.f32a.next()
            self.tsadd("dve", d[:, 0:n], self.pd[:, c0:c0 + n], esink_ap, [self.B_pd, self.B_esink], [B_d])
            self.stt("dve", t[:, 0:n], t[:, 0:n], 1.0, d[:, 0:n], ALU.add, ALU.mult, [B_t, B_d], [B_t])
        else:
            self.stt("dve", t[:, 0:n], t[:, 0:n], 1.0, self.pd[:, c0:c0 + n], ALU.add, ALU.mult,
                     [B_t, self.B_pd], [B_t])
        self.act(t[:, 0:n], t[:, 0:n], AF.Ln, [B_t], [B_t])
        self.act(t[:, 0:n], t[:, 0:n], AF.Exp, [B_t], [B_t], scale=-1.0)
        self.tt("dve", t[:, 0:n], t[:, 0:n], pz[:, 0:n], ALU.mult, [B_t, B_pz], [B_t])
        self.tt("dve", self.gT[:, chunk, c0:c0 + n], t[:, 0:n], o32[:, 0:n], ALU.mult, [B_t, B_o32], [self.B_gT])

    def out_proj(self, names):
        T = self.T
        for si, name in enumerate(names):
            wv, B_w = self.slab(name)
            for oc in range(4):
                pz, B_pz = self.proj_fm(wv, B_w, oc * 128, 128, T, 8, lambda k: self.gT[:, k, 0:T], [self.B_gT])
                c = si * 4 + oc
                self.tt("dve", self.xT[:, c, 0:T], self.xT[:, c, 0:T], pz[:, 0:T], ALU.add, [self.B_xT, B_pz],
                        [self.B_xT])

    def pzb(self):
        pz, B = self.pz.next()
        return pz[:, :].bitcast(BF16), B

    def layer_a(self, j):
        S = self.S
        T, kind, bi = self.T, self.kind, self.bi
        cur = bi % 2 if kind == "p" else 0
        prev = 1 - cur
        want_out = (kind == "s") or (bi == NPB - 1)
        scale = 128.0 ** -0.5
        self.rms_xn(8 * j)
        AK, AV = self.AK[j], self.AV[j]
        B_AKc, B_AVc = self.B_AK[j][cur], self.B_AV[j][cur]
        B_AKp, B_AVp = self.B_AK[j][prev], self.B_AV[j][prev]
        xn_fn = lambda k: self.xn[:, k, 0:T]
        qkpipe = Pipe(1)
        for which in range(2):
            gcol = 32 + 2 * j + which
            for sl in range(2):
                wv, B_w = self.slab("A%d_in%d" % (j, which * 1024 + sl * 512))
                for hh in range(4):
                    h = sl * 4 + hh
                    pz, B_pz = self.proj_fm(wv, B_w, hh * 128, 128, T, 8, xn_fn, [self.B_xn])

                    def stage2(pz=pz, B_pz=B_pz, h=h, which=which, gcol=gcol):
                        r, B_r = self.sumsq_rstd(pz, B_pz, 0, 128, T, 0, 1.0 / 128)
                        if which == 0:
                            self.stt("dve", self.QT[:, h, 0:T], pz[:, 0:T], self.gp[:, gcol:gcol + 1], r[:, 0:T],
                                     ALU.mult, ALU.mult, [B_pz, self.B_gp, B_r], [self.B_QT])
                        elif not want_out:
                            self.stt("dve", AK[:, h, cur * TB:cur * TB + T], pz[:, 0:T], self.gp[:, gcol:gcol + 1],
                                     r[:, 0:T], ALU.mult, ALU.mult, [B_pz, self.B_gp, B_r], [B_AKc])
                        else:
                            k32, B_k32 = self.kst32.next()
                            self.stt("dve", k32[:, 0:T], pz[:, 0:T], self.gp[:, gcol:gcol + 1], r[:, 0:T],
                                     ALU.mult, ALU.mult, [B_pz, self.B_gp, B_r], [B_k32])
                            self.copy("act", AK[:, h, cur * TB:cur * TB + T], k32[:, 0:T], [B_k32], [B_AKc])
                            self.emit_k_out_a(j, h, k32, B_k32)

                    qkpipe.push(stage2)
        for sl in range(2):
            wv, B_w = self.slab("A%d_in%d" % (j, 2048 + sl * 512))
            if sl == 0:
                vfirst = True
            ntile = 4 if kind == "p" else NS
            rows = 128 if kind == "p" else 64
            for tb in range(ntile):
                pz, B_pz = self.pz.next()
                for k in range(8):
                    self.mm(pz[0:rows, :], self.xn[:, k, tb * rows:(tb + 1) * rows], wv[:, k, :], k == 0, k == 7,
                            [self.B_xn, B_w], [B_pz], inc=(k == 7))
                qkpipe.flush()
                self.copy("dve" if tb % 2 == 0 else "act", AV[0:rows, cur * 4 + tb, sl * 512:(sl + 1) * 512],
                          pz[0:rows, :], [B_pz], [B_AVc])
                if want_out:
                    os_, B_os = self.stage.next()
                    self.copy("dve", os_[0:rows, 0:512], pz[0:rows, :], [B_pz], [B_os])
                    if kind == "p":
                        dst = self.av_p[j, tb * 128:(tb + 1) * 128, sl * 512:(sl + 1) * 512]
                    else:
                        dst = self.av_s[j, tb, 448:512, sl * 512:(sl + 1) * 512]
                    S.dma("sp", dst, os_[0:rows, 0:512], reads=[B_os], final=True)
        self.mark("  A attn")
        pipe = Pipe(3)

        def load_eb(h):
            eb, B_eb = self.eba.next()
            S.dma("sp", eb[:, :], self.EBA_d[j, h, :, :], reads=[self.B_EBA], writes=[B_eb])
            return eb, B_eb

        if kind == "p":
            nxt_eb = load_eb(0)
            for h in range(8):
                if h % 4 == 0:
                    gw, B_gw = self.slab("A%d_g%d" % (j, 3072 + (h // 4) * 512))
                eb, B_eb = nxt_eb
                if h < 7:
                    nxt_eb = load_eb(h + 1)
                G = self.gate_pre(gw, B_gw, (h % 4) * 128, 0, T)
                tiles = [("c", 0)]
                if bi > 0:
                    tiles += [("p", a) for a in range(4)]
                tiles += [("c", a) for a in range(1, 4)]
                for ti, (w_, a) in enumerate(tiles):
                    if w_ == "c":
                        N, q_lo, r0, slot, Bk, Bv = TB - 128 * a, 128 * a, 0, cur, B_AKc, B_AVc
                    else:
                        N, q_lo, r0, slot, Bk, Bv = 128 * (a + 1), 0, TB - 128 * a, prev, B_AKp, B_AVp
                    lastt = ti == len(tiles) - 1
                    after = (lambda G=G, h=h: self.gate_post(G, h, 0, T)) if lastt else None
                    self.attn_tile(pipe, AK[:, h, slot * TB + a * 128: slot * TB + (a + 1) * 128], 128,
                                   self.QT[:, h, q_lo:q_lo + N], N, scale,
                                   AV[:, slot * 4 + a, h * 128:(h + 1) * 128], 0, 128, q_lo,
                                   eb[:, r0:r0 + N], [Bk], [Bv], [B_eb], ti == 0, lastt, after=after)
            pipe.flush()
        else:
            gws = [None, None]
            for s in range(NS):
                rj = j if s % 2 == 0 else 1 - j
                CK, CV = self.AK[rj], self.AV[rj]
                B_CK, B_CV = self.B_AK[rj][1], self.B_AV[rj][1]
                self.load_cache_a(j, s, CK, CV, B_CK, B_CV)
                nxt_eb = load_eb(0)
                for h in range(8):
                    if s == 0 and h % 4 == 0:
                        gws[h // 4] = self.slab("A%d_g%d" % (j, 3072 + (h // 4) * 512), ahead=1 - h // 4)
                    eb, B_eb = nxt_eb
                    if h < 7:
                        nxt_eb = load_eb(h + 1)
                    gw, B_gw = gws[h // 4]
                    G = self.gate_pre(gw, B_gw, (h % 4) * 128, s * 64, 64)
                    q_ap = self.QT[:, h, s * 64:(s + 1) * 64]
                    for a in range(5):
                        if a < 4:
                            kt = CK[:, h, TB + a * 128: TB + (a + 1) * 128]
                            v = CV[:, 4 + a, h * 128:(h + 1) * 128]
                            nk, r0, Bk, Bv = 128, TB - 128 * a, B_CK, B_CV
                        else:
                            kt = AK[:, h, s * 64:(s + 1) * 64]
                            v = AV[0:64, s, h * 128:(h + 1) * 128]
                            nk, r0, Bk, Bv = 64, 0, B_AKc, B_AVc
                        after = (lambda G=G, h=h, s=s: self.gate_post(G, h, s * 64, 64)) if a == 4 else None
                        self.attn_tile(pipe, kt, nk, q_ap, 64, scale, v, 0, 128, s * 64, eb[0:nk, r0:r0 + 64],
                                       [Bk], [Bv], [B_eb], a == 0, a == 4, after=after)
                pipe.flush()
        self.mark("  A out")
        self.out_proj(["A%d_o0" % j, "A%d_o512" % j])

    def emit_k_out_a(self, j, h, k32, B_k32):
        S = self.S
        ko_, B_ko = self.stage.next()
        ko = ko_[:, 0:512].rearrange("p (a d) -> p a d", a=4)
        pz, B_pz = self.pz.next()
        if self.kind == "p":
            for tb in range(4):
                self.tr(pz[:, tb * 128:(tb + 1) * 128], k32[:, tb * 128:(tb + 1) * 128], self.ident[:, :],
                        [B_k32, self.B_ident], [B_pz], inc=(tb == 3))
            self.copy("act", ko[:, :, :], pz[:, :].rearrange("p (a d) -> p a d", a=4), [B_pz], [B_ko])
            S.dma("sp", self.ak_p[j, :, h * 128:(h + 1) * 128].rearrange("(a p) d -> p a d", p=128), ko[:, :, :],
                  reads=[B_ko], final=True)
        else:
            for s in range(NS):
                self.tr(pz[0:64, s * 128:(s + 1) * 128], k32[:, s * 64:(s + 1) * 64], self.ident[:, :],
                        [B_k32, self.B_ident], [B_pz], inc=(s == NS - 1))
            self.copy("act", ko[0:64, :, :], pz[0:64, :].rearrange("p (a d) -> p a d", a=4), [B_pz], [B_ko])
            S.dma("sp", self.ak_s[j, :, 448:512, h * 128:(h + 1) * 128].rearrange("s p d -> p s d"), ko[0:64, :, :],
                  reads=[B_ko], final=True)

    def load_cache_a(self, j, s, CK, CV, B_CK, B_CV):
        S = self.S
        if "nod2d" not in DBG:
            S.dma("sp", self.ak_s[j, s, 0:448, :], self.cak[j, s, 64:512, :], semkey="d2d", final=True)
            S.dma("sp", self.av_s[j, s, 0:448, :], self.cav[j, s, 64:512, :], semkey="d2d", final=True)
        S.dma("pool", CV[:, 4:8, :], self.cav[j, s, :, :].rearrange("(a p) d -> p a d", p=128), writes=[B_CV])
        for hf in range(2):
            ct_, B_ct = self.sc4.next()
            ct = ct_[:, :].rearrange("p (a d) -> p a d", a=2)
            S.dma("pool", ct[:, :, :],
                  self.cak[j, s, hf * 256:(hf + 1) * 256, :].rearrange("(a p) d -> p a d", p=128), writes=[B_ct])
            for hp in range(4):
                pt, B_pt = self.pzb()
                for hh in range(2):
                    h = hp * 2 + hh
                    for a in range(2):
                        self.tr(pt[:, hh * 256 + a * 128: hh * 256 + (a + 1) * 128], ct[:, a, h * 128:(h + 1) * 128],
                                self.identb[:, :], [B_ct, self.B_identb], [B_pt], inc=(hh == 1 and a == 1))
                self.copy("act" if hp % 2 == 0 else "dve",
                          CK[:, hp * 2:hp * 2 + 2, TB + hf * 256: TB + (hf + 1) * 256],
                          pt[:, 0:512].rearrange("p (h t) -> p h t", h=2), [B_pt], [B_CK])

    def layer_b(self):
        S = self.S
        T, kind, bi = self.T, self.kind, self.bi
        cur = bi % 2 if kind == "p" else 0
        prev = 1 - cur
        want_out = (kind == "s") or (bi == NPB - 1)
        scale = 64.0 ** -0.5
        self.rms_xn(16)
        BK, BV = self.BK, self.BV
        B_BKc, B_BVc, B_BKp, B_BVp = self.B_BK[cur], self.B_BV[cur], self.B_BK[prev], self.B_BV[prev]
        xn_fn = lambda k: self.xn[:, k, 0:T]
        qkpipe = Pipe(1)
        for sl in range(2):
            wv, B_w = self.slab("B_in%d" % (sl * 512))
            for cc in range(4):
                c = sl * 4 + cc
                pz, B_pz = self.proj_fm(wv, B_w, cc * 128, 128, T, 8, xn_fn, [self.B_xn])

                def stage2(pz=pz, B_pz=B_pz, c=c):
                    r, B_r = self.sumsq_rstd(pz, B_pz, 0, 128, T, 1, 1.0 / 64)
                    self.stt("dve", self.QT[:, c, 0:T], pz[:, 0:T], self.gp[:, 36:37], r[:, 0:T], ALU.mult, ALU.mult,
                             [B_pz, self.B_gp, B_r], [self.B_QT])

                qkpipe.push(stage2)
        wv, B_w = self.slab("B_in1024")
        if want_out:
            qkpipe.flush()
            kst_, B_kst = self.stage.next()
        for g in range(4):
            pzp = self.pz.next()
            for half in range(2):
                self.proj_fm(wv, B_w, g * 64, 64, T, 8, xn_fn, [self.B_xn], orow=half * 64, pz_pair=pzp)
            pz, B_pz = pzp
            if not want_out:
                def stage2(pz=pz, B_pz=B_pz, g=g):
                    r, B_r = self.sumsq_rstd(pz, B_pz, 0, 128, T, 1, 1.0 / 64)
                    self.stt("dve", BK[:, g, cur * TB:cur * TB + T], pz[:, 0:T], self.gp[:, 37:38], r[:, 0:T],
                             ALU.mult, ALU.mult, [B_pz, self.B_gp, B_r], [B_BKc])

                qkpipe.push(stage2)
            else:
                r, B_r = self.sumsq_rstd(pz, B_pz, 0, 128, T, 1, 1.0 / 64)
                k32, B_k32 = self.kst32.next()
                self.stt("dve", k32[:, 0:T], pz[:, 0:T], self.gp[:, 37:38], r[:, 0:T], ALU.mult, ALU.mult,
                         [B_pz, self.B_gp, B_r], [B_k32])
                self.copy("act", BK[:, g, cur * TB:cur * TB + T], k32[:, 0:T], [B_k32], [B_BKc])
                pt, B_pt = self.pz.next()
                if kind == "p":
                    self.tr(pt[:, 0:64], k32[0:64, 384:512], self.ident[0:64, 0:64], [B_k32, self.B_ident], [B_pt])
                    self.copy("act", kst_[:, g * 64:(g + 1) * 64], pt[:, 0:64], [B_pt], [B_kst])
                else:
                    for s in range(NS):
                        self.tr(pt[0:64, s * 64:(s + 1) * 64], k32[0:64, s * 64:(s + 1) * 64], self.ident[0:64, 0:64],
                                [B_k32, self.B_ident], [B_pt], inc=(s == NS - 1))
                    self.copy("act", kst_[0:64, 0:1024].rearrange("p (s c) -> p s c", s=4)[:, :, g * 64:(g + 1) * 64],
                              pt[0:64, 0:256].rearrange("p (s d) -> p s d", s=4), [B_pt], [B_kst])
        if want_out:
            if kind == "p":
                S.dma("sp", self.bk_p[:, :], kst_[:, 0:256], reads=[B_kst], final=True)
            else:
                S.dma("sp", self.bk_s[:, 64:128, :].rearrange("s p c -> p s c"),
                      kst_[0:64, 0:1024].rearrange("p (s c) -> p s c", s=4), reads=[B_kst], final=True)
        ntile = 4 if kind == "p" else NS
        rows = 128 if kind == "p" else 64
        for tb in range(ntile):
            pz, B_pz = self.pz.next()
            for k in range(8):
                self.mm(pz[0:rows, 0:256], self.xn[:, k, tb * rows:(tb + 1) * rows], wv[:, k, 256:512], k == 0, k == 7,
                        [self.B_xn, B_w], [B_pz], inc=(k == 7))
            qkpipe.flush()
            self.copy("act", BV[0:rows, cur * 4 + tb, :], pz[0:rows, 0:256], [B_pz], [B_BVc])
            if want_out and (kind == "s" or tb == 3):
                os_, B_os = self.stage.next()
                self.copy("dve", os_[0:rows, 0:256], pz[0:rows, 0:256], [B_pz], [B_os])
                dst = self.bv_p[:, :] if kind == "p" else self.bv_s[tb, 64:128, :]
                S.dma("sp", dst, os_[0:rows, 0:256], reads=[B_os], final=True)
        if kind == "s":
            S.dma("sp", self.bk_s[:, 0:64, :], self.cbk[:, 64:128, :], semkey="d2d", final=True)
            S.dma("sp", self.bv_s[:, 0:64, :], self.cbv[:, 64:128, :], semkey="d2d", final=True)
            S.dma("pool", BV[:, prev * 4:prev * 4 + 4, :], self.cbv[:, :, :].rearrange("s p d -> p s d"),
                  writes=[B_BVp])
            ct, B_ct = self.cst.next()
            S.dma("pool", ct[:, :, :], self.cbk[:, :, :].rearrange("s p d -> p s d"), writes=[B_ct])
            for g in range(4):
                pt, B_pt = self.pzb()
                for s in range(NS):
                    for half in range(2):
                        self.tr(pt[half * 64:(half + 1) * 64, s * 128:(s + 1) * 128], ct[:, s, g * 64:(g + 1) * 64],
                                self.identb[:, :], [B_ct, self.B_identb], [B_pt], inc=(s == NS - 1 and half == 1))
                self.copy("act", BK[:, g, prev * TB:(prev + 1) * TB], pt[:, 0:512], [B_pt], [B_BKp])
        self.mark("  B attn")
        pipe = Pipe(3)

        def load_eb(h):
            eb, B_eb = self.eba.next()
            S.dma("sp", eb[:, 0:256], self.EBB_d[h, :, :], reads=[self.B_EBB], writes=[B_eb])
            return eb, B_eb

        nxt_eb = load_eb(0)
        for c in range(8):
            if c % 4 == 0:
                gw, B_gw = self.slab("B_g%d" % (1536 + (c // 4) * 512))
            G = self.gate_pre(gw, B_gw, (c % 4) * 128, 0, T)
            for half in range(2):
                h = 2 * c + half
                g = h // 4
                orow = half * 64
                eb, B_eb = nxt_eb
                if h < 15:
                    nxt_eb = load_eb(h + 1)
                tl = []
                if kind == "p":
                    tiles = []
                    if bi > 0:
                        tiles.append((prev, 3, 0, 128, 128))
                    for a in range(4):
                        tiles.append((cur, a, 128 * a, min(256, T - 128 * a), 0))
                    for ti, (slot, a, q_lo, N, r0) in enumerate(tiles):
                        Bk, Bv = (B_BKc, B_BVc) if slot == cur else (B_BKp, B_BVp)
                        tl.append((BK[orow:orow + 64, g, slot * TB + a * 128: slot * TB + (a + 1) * 128], 128,
                                   self.QT[orow:orow + 64, c, q_lo:q_lo + N], N,
                                   BV[:, slot * 4 + a, g * 64:(g + 1) * 64], q_lo, eb[:, r0:r0 + N], Bk, Bv,
                                   ti == len(tiles) - 1))
                else:
                    for s in range(NS):
                        q_ap = self.QT[orow:orow + 64, c, s * 64:(s + 1) * 64]
                        tl.append((BK[orow:orow + 64, g, prev * TB + s * 128: prev * TB + (s + 1) * 128], 128, q_ap, 64,
                                   BV[:, prev * 4 + s, g * 64:(g + 1) * 64], s * 64, eb[:, 128:192], B_BKp, B_BVp,
                                   False))
                        tl.append((BK[orow:orow + 64, g, cur * TB + s * 64: cur * TB + (s + 1) * 64], 64, q_ap, 64,
                                   BV[0:64, cur * 4 + s, g * 64:(g + 1) * 64], s * 64, eb[0:64, 0:64], B_BKc, B_BVc,
                                   s == NS - 1))
                for ti, (kt, nk, q_ap, N, v, q_lo, eb_ap, Bk, Bv, lastt) in enumerate(tl):
                    zero = T if (half == 0 and ti == 0) else None
                    after = None
                    if half == 1 and ti == len(tl) - 1:
                        after = lambda G=G, c=c: self.gate_post(G, c, 0, T, esink_ap=self.esink[:, c:c + 1])
                    self.attn_tile(pipe, kt, nk, q_ap, N, scale, v, orow, 64, q_lo, eb_ap, [Bk], [Bv], [B_eb],
                                   False, lastt, zero=zero, after=after)
        pipe.flush()
        self.mark("  B out")
        self.out_proj(["B_o0", "B_o512"])

    def c_expand(self, kvb, B_kvb, ckv_fn, ckv_bufs, kr_ap, kr_bufs, n, kdst_fn, vdst, B_kdst, B_vdst):
        S = self.S
        kpipe = Pipe(1)
        for h in range(16):
            pz, B_pz = self.proj_fm(kvb, B_kvb, h * 128, 64, n, 2, ckv_fn, ckv_bufs)

            def stage2(pz=pz, B_pz=B_pz, h=h):
                r, B_r = self.sumsq_rstd(pz, B_pz, 0, 64, n, 1, 1.0 / 64)
                kt, B_kt = self.kts.next()
                self.stt("dve", kt[0:64, 0:n], pz[0:64, 0:n], self.gp[0:64, 46:47], r[0:64, 0:n], ALU.mult, ALU.mult,
                         [B_pz, self.B_gp, B_r], [B_kt])
                self.copy("pool", kt[64:96, 0:n], kr_ap, kr_bufs, [B_kt])
                S.dma("sp", kdst_fn(h), kt[0:96, 0:n], reads=[B_kt], writes=[B_kdst])

            kpipe.push(stage2)
        kpipe.flush()
        vs_, B_vs = self.sc4.next()
        ntb = (n + 127) // 128
        kv3 = kvb[:, :, :].rearrange("p k (h e) -> p k h e", e=128)
        for hf in range(2):
            vs = vs_[:, :].rearrange("p (h a d) -> p h a d", h=8, a=4)
            for tb in range(ntb):
                rows = min(128, n - tb * 128)
                pz, B_pz = self.pz.next()
                for k in range(2):
                    self.mm(pz[0:rows, :].rearrange("p (h d) -> p h d", h=8), ckv_fn(k)[:, tb * 128: tb * 128 + rows],
                            kv3[:, k, hf * 8:(hf + 1) * 8, 64:128], k == 0, k == 1, ckv_bufs + [B_kvb], [B_pz],
                            inc=(k == 1))
                self.copy("act", vs[0:rows, :, tb, :], pz[0:rows, :].rearrange("p (h d) -> p h d", h=8), [B_pz], [B_vs])
            rows = min(128, n)
            S.dma("sp", vdst(hf, rows, ntb), vs[0:rows, :, 0:ntb, :], reads=[B_vs], writes=[B_vdst])

    def layer_c(self):
        S = self.S
        T, kind, bi = self.T, self.kind, self.bi
        scale = 96.0 ** -0.5
        self.rms_xn(24)
        xn_fn = lambda k: self.xn[:, k, 0:T]
        big = self.big
        csrc = self.csP[:, :, bi * TB: bi * TB + T] if kind == "p" else self.csS[:, :, 0:T]
        S.dma("sp", self.cs[64:96, :, 0:T], csrc.rearrange("a r t -> r a t"), writes=[self.B_cs])
        wv, B_w = self.slab("C_qa")
        pss, B_pss = self.pss.next()
        for oc in range(4):
            pz, B_pz = self.proj_fm(wv, B_w, oc * 128, 128, T, 8, xn_fn, [self.B_xn])
            sq, B_sq = self.sq.next()
            self.act(sq[:, 0:T], pz[:, 0:T], AF.Square, [B_pz], [B_sq])
            self.copy("dve", big[:, oc, 0:T], pz[:, 0:T], [B_pz], [self.B_QT])
            self.mm(pss[:, 0:T], self.bd[:, 0, :], sq[:, 0:T], oc == 0, oc == 3, [self.B_bd, B_sq], [B_pss],
                    inc=(oc == 3))
        r, B_r = self.rstd_from(pss, B_pss, 0, 128, T, 1.0 / 512)
        for oc in range(4):
            self.stt("dve", self.qan[:, oc, 0:T], big[:, oc, 0:T], self.gp[:, 38 + oc:39 + oc], r[:, 0:T],
                     ALU.mult, ALU.mult, [self.B_QT, self.B_gp, B_r], [self.B_qan])
        wv, B_w = self.slab("C_kvr")
        self.copy("pool", self.wsk[:, :, 0:16], wv[:, :, 272:288], [B_w], [self.B_wsk])
        self.copy("pool", self.wsk[:, :, 16:32], wv[:, :, 256:272], [B_w], [self.B_wsk])
        pss, B_pss = self.pss.next()
        for oc in range(2):
            pz, B_pz = self.proj_fm(wv, B_w, oc * 128, 128, T, 8, xn_fn, [self.B_xn])
            sq, B_sq = self.sq.next()
            self.act(sq[:, 0:T], pz[:, 0:T], AF.Square, [B_pz], [B_sq])
            self.copy("dve", big[:, oc, 0:T], pz[:, 0:T], [B_pz], [self.B_QT])
            self.mm(pss[:, 0:T], self.bd[:, 0, :], sq[:, 0:T], oc == 0, oc == 1, [self.B_bd, B_sq], [B_pss],
                    inc=(oc == 1))
        r, B_r = self.rstd_from(pss, B_pss, 0, 128, T, 1.0 / 256)
        for oc in range(2):
            self.stt("dve", big[:, 2 + oc, 0:T], big[:, oc, 0:T], self.gp[:, 42 + oc:43 + oc], r[:, 0:T],
                     ALU.mult, ALU.mult, [self.B_QT, self.B_gp, B_r], [self.B_QT])
            self.copy("act", self.ckvT[:, oc, 0:T], big[:, 2 + oc, 0:T], [self.B_QT], [self.B_ckvT])
        dst_kv = self.ckv_p if kind == "p" else self.ckv_s
        dst_kr = self.ckr_p if kind == "p" else self.ckr_s
        row0 = bi * TB if kind == "p" else 0
        for tb in range(T // 128):
            pz, B_pz = self.pz.next()
            for oc in range(2):
                self.tr(pz[:, oc * 128:(oc + 1) * 128], big[:, 2 + oc, tb * 128:(tb + 1) * 128], self.ident[:, :],
                        [self.B_QT, self.B_ident], [B_pz], inc=(oc == 1))
            os_, B_os = self.stage.next()
            self.copy("act", os_[:, 0:256], pz[:, 0:256], [B_pz], [B_os])
            S.dma("sp", dst_kv[row0 + tb * 128: row0 + (tb + 1) * 128, :], os_[:, 0:256], reads=[B_os], final=True)
        pzA, B_pzA = self.proj_fm(wv, B_w, 256, 32, T, 8, xn_fn, [self.B_xn], orow=64)
        pzB, B_pzB = self.proj_fm(self.wsk, self.B_wsk, 0, 32, T, 8, xn_fn, [self.B_xn], orow=64)
        r, B_r = self.sumsq_rstd(pzA, B_pzA, 64, 32, T, 2, 1.0 / 32)
        kr32, B_kr32 = self.f32b.next()
        self.rope_rows(kr32, B_kr32, pzA, B_pzA, pzB, B_pzB, r, B_r, 47, 56, T)
        self.copy("act", self.krb[64:96, 0:T], kr32[64:96, 0:T], [B_kr32], [self.B_krb])
        for tb in range(T // 128):
            pz, B_pz = self.pz.next()
            self.tr(pz[:, 0:32], kr32[64:96, tb * 128:(tb + 1) * 128], self.ident[64:96, 64:96],
                    [B_kr32, self.B_ident], [B_pz])
            os_, B_os = self.stage.next()
            self.copy("act", os_[:, 0:32], pz[:, 0:32], [B_pz], [B_os])
            S.dma("sp", dst_kr[row0 + tb * 128: row0 + (tb + 1) * 128, :], os_[:, 0:32], reads=[B_os], final=True)
        self.mark("  C expand")
        kvb, B_kvb = self.slab("C_kvb")
        if kind == "p":
            self.c_expand(kvb, B_kvb, lambda k: self.ckvT[:, k, 0:T], [self.B_ckvT], self.krb[64:96, 0:T],
                          [self.B_krb], T, lambda h: self.KCp[h, :, bi * TB:(bi + 1) * TB],
                          lambda hf, rows, ntb: self.VCp[hf * 8:(hf + 1) * 8, bi, :, :].rearrange(
                              "h p (a d) -> p h a d", d=64),
                          self.B_KCp[bi], self.B_VCp[bi])
        else:
            for s in range(NS):
                for g in range(4):
                    ct, B_ct = self.cst.next()
                    S.dma("pool", ct[:, :, :],
                          self.cckv[s, g * 512:(g + 1) * 512, :].rearrange("(a p) d -> p a d", p=128), writes=[B_ct])
                    ct2, B_ct2 = self.cst2.next()
                    S.dma("pool", ct2[:, :, :],
                          self.cckr[s, g * 512:(g + 1) * 512, :].rearrange("(a p) d -> p a d", p=128), writes=[B_ct2])
                    for kc in range(2):
                        pt, B_pt = self.pzb()
                        for a in range(4):
                            self.tr(pt[:, a * 128:(a + 1) * 128], ct[:, a, kc * 128:(kc + 1) * 128], self.identb[:, :],
                                    [B_ct, self.B_identb], [B_pt], inc=(a == 3))
                        self.copy("act" if kc == 0 else "dve", self.ckc[:, kc, :], pt[:, 0:512], [B_pt], [self.B_ckc])
                    pt, B_pt = self.pzb()
                    for a in range(4):
                        self.tr(pt[64:96, a * 128:(a + 1) * 128], ct2[:, a, :], self.identb[:, :],
                                [B_ct2, self.B_identb], [B_pt], inc=(a == 3))
                    self.copy("act", self.krc[64:96, :], pt[64:96, 0:512], [B_pt], [self.B_krc])
                    self.c_expand(kvb, B_kvb, lambda k: self.ckc[:, k, :], [self.B_ckc], self.krc[64:96, :],
                                  [self.B_krc], 512, lambda h, s=s, g=g: self.KCs[s, h, :, g * 512:(g + 1) * 512],
                                  lambda hf, rows, ntb, s=s, g=g: self.VCs[s, hf * 8:(hf + 1) * 8, g, :, :].rearrange(
                                      "h p (a d) -> p h a d", d=64),
                                  self.B_KCs[s][g], self.B_VCs[s][g])
                self.c_expand(kvb, B_kvb, lambda k, s=s: self.ckvT[:, k, s * 64:(s + 1) * 64], [self.B_ckvT],
                              self.krb[64:96, s * 64:(s + 1) * 64], [self.B_krb], 64,
                              lambda h, s=s: self.KCs[s, h, :, 2048:2112],
                              lambda hf, rows, ntb, s=s: self.VCs[s, hf * 8:(hf + 1) * 8, 4, 0:64, 0:64].rearrange(
                                  "h p (a d) -> p h a d", d=64),
                              self.B_KCs[s][4], self.B_VCs[s][4])
        self.mark("  C qproj")
        hbase = 0
        qpipe = Pipe(1)
        pz_saved = self.pz
        self.pz = Rot(pz_saved.items + self.pst.items)
        for name, nh in (("C_qb0", 10), ("C_qb1", 6)):
            wv, B_w = self.slab(name)
            w4 = wv[:, :, 0:nh * 96].rearrange("p k (h e) -> p k h e", e=96)
            self.copy("pool", self.wsw[:, :, 0:nh, 0:16], w4[:, :, :, 80:96], [B_w], [self.B_wsw])
            self.copy("pool", self.wsw[:, :, 0:nh, 16:32], w4[:, :, :, 64:80], [B_w], [self.B_wsw])
            qan_fn = lambda k: self.qan[:, k, 0:T]
            for hh in range(nh):
                h = hbase + hh
                pzA, B_pzA = self.proj_fm(wv, B_w, hh * 96, 96, T, 4, qan_fn, [self.B_qan])
                pzB, B_pzB = self.pz.next()
                for k in range(4):
                    self.mm(pzB[64:96, 0:T], self.wsw[:, k, hh, :], self.qan[:, k, 0:T], k == 0, k == 3,
                            [self.B_wsw, self.B_qan], [B_pzB], inc=(k == 3))

                def stage2(pzA=pzA, B_pzA=B_pzA, pzB=pzB, B_pzB=B_pzB, h=h):
                    r, B_r = self.sumsq_rstd(pzA, B_pzA, 0, 96, T, 2, self.cpk[0:96, 0:1])
                    self.stt("dve", self.QT[0:64, h, 0:T], pzA[0:64, 0:T], self.gp[0:64, 44:45], r[0:64, 0:T],
                             ALU.mult, ALU.mult, [B_pzA, self.B_gp, B_r], [self.B_QT])
                    t32, B_t32 = self.f32b.next()
                    self.rope_rows(t32, B_t32, pzA, B_pzA, pzB, B_pzB, r, B_r, 44, 45, T)
                    self.copy("act", self.QT[64:96, h, 0:T], t32[64:96, 0:T], [B_t32], [self.B_QT])

                qpipe.push(stage2)
            qpipe.flush()
            hbase += nh
        self.pz = pz_saved
        self.mark("  C attn")
        pipe = Pipe(3)
        items = []
        for h in range(16):
            if kind == "p":
                for kg in range(bi + 1):
                    items.append((h, ("p", kg)))
            else:
                for s in range(NS):
                    for g in range(5):
                        items.append((h, ("s", s, g)))
        loaded = {}

        def issue(ii):
            h, d = items[ii]
            kc, B_kc = self.kcg.next()
            vc, B_vc = self.vcg.next()
            if d[0] == "p":
                kg = d[1]
                S.dma("sp", kc[0:96, :], self.KCp[h, :, kg * TB:(kg + 1) * TB], reads=[self.B_KCp[kg]], writes=[B_kc])
                S.dma("sp", vc[:, :, 64:128], self.VCp[h, kg, :, :].rearrange("p (a d) -> p a d", d=64),
                      reads=[self.B_VCp[kg]], writes=[B_vc])
            else:
                s_, g = d[1], d[2]
                nkeys = 512 if g < 4 else 64
                S.dma("sp", kc[0:96, 0:nkeys], self.KCs[s_, h, :, g * 512: g * 512 + nkeys],
                      reads=[self.B_KCs[s_][g]], writes=[B_kc])
                if g < 4:
                    S.dma("sp", vc[:, :, 64:128], self.VCs[s_, h, g, :, :].rearrange("p (a d) -> p a d", d=64),
                          reads=[self.B_VCs[s_][g]], writes=[B_vc])
                else:
                    S.dma("sp", vc[0:64, 0, 64:128], self.VCs[s_, h, 4, 0:64, 0:64], reads=[self.B_VCs[s_][4]],
                          writes=[B_vc])
            loaded[ii] = (kc, B_kc, vc, B_vc)

        n_issued = 0
        G = None
        for ii, (h, d) in enumerate(items):
            if ii > 0 and items[ii - 1][1][0] == "s" and items[ii - 1][1][2] == 4:
                pipe.flush()
            while n_issued < min(len(items), ii + 2):
                issue(n_issued)
                n_issued += 1
            kc, B_kc, vc, B_vc = loaded.pop(ii)
            c = h // 2
            orow = (h % 2) * 64
            first_of_head = ii == 0 or items[ii - 1][0] != h
            last_of_head = ii == len(items) - 1 or items[ii + 1][0] != h
            if first_of_head and h % 8 == 0:
                gw, B_gw = self.slab("C_g%d" % (800 + (h // 8) * 512))
            if first_of_head and h % 2 == 0:
                G = self.gate_pre(gw, B_gw, (c % 4) * 128, 0, T)
            vlo = 64 if h % 2 == 0 else 0
            acc = (self.po, self.B_po) if h % 2 == 0 else (self.pd, self.B_pd)
            if d[0] == "p":
                diag = d[1] == bi
                tl = []
                for a in range(4):
                    q_lo = 128 * a if diag else 0
                    N = T - q_lo
                    tl.append((kc[0:96, a * 128:(a + 1) * 128], 128, self.QT[0:96, h, q_lo:q_lo + N], N,
                               vc[:, a, vlo:vlo + 128], q_lo, self.maskc[:, 0:N] if diag else None, diag and a == 3))
            else:
                s_, g = d[1], d[2]
                q_ap = self.QT[0:96, h, s_ * 64:(s_ + 1) * 64]
                nk = 128 if g < 4 else 64
                tl = [(kc[0:96, a * 128: a * 128 + nk], nk, q_ap, 64, vc[0:nk, a, vlo:vlo + 128], s_ * 64, None,
                       g == 4 and s_ == NS - 1) for a in range(4 if g < 4 else 1)]
            for ti, (kt, nk, q_ap, N, v, q_lo, eb_ap, lastt) in enumerate(tl):
                after = None
                if last_of_head and h % 2 == 1 and ti == len(tl) - 1:
                    after = lambda G=G, c=c: self.gate_post_c(G, c, T)
                self.attn_tile(pipe, kt, nk, q_ap, N, scale, v, 0, 128, q_lo, eb_ap, [B_kc], [B_vc],
                               [self.B_maskc] if eb_ap is not None else [], first_of_head and ti == 0, lastt,
                               after=after, fused_acc=acc)
        pipe.flush()
        self.mark("  C out")
        self.out_proj(["C_o0", "C_o512"])

    def rope_rows(self, o32, B_o, pzA, B_pzA, pzB, B_pzB, r, B_r, gcol, gcol_sw, T):
        t2, B_t2 = self.f32a.next()
        self.stt("dve", o32[64:96, 0:T], pzA[64:96, 0:T], self.gp[64:96, gcol:gcol + 1], self.cs[64:96, 0, 0:T],
                 ALU.mult, ALU.mult, [B_pzA, self.B_gp, self.B_cs], [B_o])
        self.stt("dve", t2[64:96, 0:T], pzB[64:96, 0:T], self.gp[64:96, gcol_sw:gcol_sw + 1], self.cs[64:96, 1, 0:T],
                 ALU.mult, ALU.mult, [B_pzB, self.B_gp, self.B_cs], [B_t2])
        self.tt("dve", o32[64:96, 0:T], o32[64:96, 0:T], t2[64:96, 0:T], ALU.add, [B_o, B_t2], [B_o])
        self.tt("dve", o32[64:96, 0:T], o32[64:96, 0:T], r[64:96, 0:T], ALU.mult, [B_o, B_r], [B_o])


def _consts():
    c = {}
    c["ident"] = np.eye(128, dtype=np.float32)
    bd = np.zeros((128, 4, 128), np.float32)
    bd[:, 0, :] = 1.0
    bd[0:64, 1, 0:64] = 1.0
    bd[64:128, 1, 64:128] = 1.0
    bd[0:64, 2, 0:64] = 1.0
    bd[64:96, 2, 64:96] = 1.0
    c["bdpack"] = bd
    cp = np.zeros((128, 8), np.float32)
    cp[0:64, 0] = 1.0 / 64
    cp[64:128, 0] = 1.0 / 32
    c["cpack"] = cp
    p = np.arange(128)[:, None]
    r = np.arange(256)[None, :]
    c["distb"] = np.abs(r - p).astype(np.float32)
    mc = np.ones((128, 512), np.float32)
    mc[64:128, 0:64] = 0.0
    c["maskc"] = mc
    half = 16
    inv = (np.float32(ROPE_THETA) ** (-np.arange(half, dtype=np.float32) / np.float32(half))).astype(np.float32)

    def tables(pos):
        ang = (pos.astype(np.float32)[None, :] * inv[:, None]).astype(np.float32)
        cos = np.cos(ang.astype(np.float64)).astype(np.float32)
        sin = np.sin(ang.astype(np.float64)).astype(np.float32)
        return np.stack([np.concatenate([cos, cos], 0), np.concatenate([-sin, sin], 0)], 0)

    c["csP"] = tables(np.arange(SEQ))
    c["csS"] = np.tile(tables(PAST + np.arange(DS)), (1, 1, NS))
    return c


def _gpack(inp):
    g = np.zeros((128, GP_NCOL), np.float32)

    def chunks(v):
        return np.asarray(v, np.float32).reshape(-1, 128).T

    g[:, 0:8] = chunks(inp["a_norm"][0])
    g[:, 8:16] = chunks(inp["a_norm"][1])
    g[:, 16:24] = chunks(inp["b_norm"][0])
    g[:, 24:32] = chunks(inp["c_norm"][0])
    g[:, 32] = inp["a_q_norm"][0]
    g[:, 33] = inp["a_k_norm"][0]
    g[:, 34] = inp["a_q_norm"][1]
    g[:, 35] = inp["a_k_norm"][1]
    g[:, 36] = np.tile(inp["b_q_norm"][0], 2)
    g[:, 37] = np.tile(inp["b_k_norm"][0], 2)
    g[:, 38:42] = chunks(inp["c_q_a_norm"][0])
    g[:, 42:44] = chunks(inp["c_kv_a_norm"][0])
    qg = np.asarray(inp["c_q_norm"][0], np.float32)
    kg = np.asarray(inp["c_k_norm"][0], np.float32)
    g[0:96, 44] = qg
    g[64:80, 45] = qg[80:96]
    g[80:96, 45] = qg[64:80]
    g[0:64, 46] = kg[0:64]
    g[64:96, 47] = kg[64:96]
    g[64:80, 56] = kg[80:96]
    g[80:96, 56] = kg[64:80]
    sk = np.asarray(inp["b_sinks"][0], np.float32)
    for c_ in range(8):
        g[0:64, 48 + c_] = sk[2 * c_]
        g[64:128, 48 + c_] = sk[2 * c_ + 1]
    return g


_NC_CACHE = {}
CFG = {"layers": (0, 1, 2, 3), "npb": NPB, "do_sample": True}


def kernel(**inp):
    inp = {k: np.asarray(v) for k, v in inp.items()}
    key = (tuple(CFG["layers"]), CFG["npb"], CFG["do_sample"])
    if key not in _NC_CACHE:
        b = Builder(layers=CFG["layers"], npb=CFG["npb"], do_sample=CFG["do_sample"])
        _NC_CACHE[key] = (b.build(), b)
    nc, b = _NC_CACHE[key]
    return run_kernel(nc, b, inp)


def make_in_maps(inp):
    consts = _consts()
    gp = _gpack(inp)
    p = np.arange(128)[:, None]
    r = np.arange(640)[None, :]
    idx = np.clip(r - p, -128, 128) + 128
    relT = np.ascontiguousarray(inp["a_rel_bias"][:, :, idx]).astype(np.float32)
    shared = {
        "a_w_in": inp["a_w_in"], "a_w_out": inp["a_w_out"], "b_w_in": inp["b_w_in"][0], "b_w_out": inp["b_w_out"][0],
        "c_w_in": inp["c_w_in"][0], "c_w_qb": inp["c_w_qb"][0], "c_w_kvb": inp["c_w_kvb"][0],
        "c_w_out": inp["c_w_out"][0], "gpack": gp, "relT": relT,
    }
    shared.update(consts)
    maps = []
    for c in range(NCORES):
        sl = slice(c * NS, (c + 1) * NS)
        m = dict(shared)
        m["xp"] = inp["x_prompt"][c]
        m["xs"] = inp["x_sample"][sl].reshape(NS * DS, D)
        m["cak"] = inp["cache_a_k"][:, sl].reshape(2, NS, 512, 1024)
        m["cav"] = inp["cache_a_v"][:, sl].reshape(2, NS, 512, 1024)
        m["cbk"] = inp["cache_b_k"][0, sl].reshape(NS, 128, 256)
        m["cbv"] = inp["cache_b_v"][0, sl].reshape(NS, 128, 256)
        m["cckv"] = inp["cache_c_kv"][0, sl]
        m["cckr"] = inp["cache_c_kr"][0, sl]
        maps.append({k: np.ascontiguousarray(v, dtype=np.float32) for k, v in m.items()})
    return maps


def run_kernel(nc, b, inp, trace=False):
    maps = make_in_maps(inp)
    maps = [{k: m[k] for k in b.in_names} for m in maps]
    res = run_bass_kernel_spmd(nc, maps, core_ids=list(range(NCORES)), trace=trace)
    R = res.results

    def cat(name, shape_per_core, axis):
        return np.stack([R[c][name].reshape(shape_per_core) for c in range(NCORES)], axis=axis)

    y_p = cat("y_p", (SEQ, D), 0)
    y_s = np.concatenate([R[c]["y_s"].reshape(NS, DS, D) for c in range(NCORES)], 0)
    ak_p = cat("ak_p", (2, 512, 8, 128), 1)
    av_p = cat("av_p", (2, 512, 8, 128), 1)
    bk_p = cat("bk_p", (128, 4, 64), 0)[None]
    bv_p = cat("bv_p", (128, 4, 64), 0)[None]
    ckv_p = cat("ckv_p", (SEQ, 256), 0)[None]
    ckr_p = cat("ckr_p", (SEQ, 32), 0)[None]
    ak_s = np.concatenate([R[c]["ak_s"].reshape(2, NS, 512, 8, 128) for c in range(NCORES)], 1)
    av_s = np.concatenate([R[c]["av_s"].reshape(2, NS, 512, 8, 128) for c in range(NCORES)], 1)
    bk_s = np.concatenate([R[c]["bk_s"].reshape(NS, 128, 4, 64) for c in range(NCORES)], 0)[None]
    bv_s = np.concatenate([R[c]["bv_s"].reshape(NS, 128, 4, 64) for c in range(NCORES)], 0)[None]
    ckv_s = np.concatenate([R[c]["ckv_s"].reshape(NS, DS, 256) for c in range(NCORES)], 0)[None]
    ckr_s = np.concatenate([R[c]["ckr_s"].reshape(NS, DS, 32) for c in range(NCORES)], 0)[None]
    outs = (y_p, y_s, ak_p, av_p, bk_p, bv_p, ckv_p, ckr_p, ak_s, av_s, bk_s, bv_s, ckv_s, ckr_s)
    outs = tuple(np.ascontiguousarray(o, dtype=np.float32) for o in outs)
    if trace:
        return outs, res
    return outs
```
